# Optimizing a Trainium2 kernel written in Bass

```python
import jax
import jax.numpy as jnp
from jax import lax
import numpy as np

D_MODEL = 1024
BATCH = 32
SEQ = 2048
DEPTH = 1

HEAD_DIM = 64
N_ATTN_HEADS = (D_MODEL // 2) // HEAD_DIM
ATTN_WIDTH = N_ATTN_HEADS * HEAD_DIM
MOBA_BLOCK = 256
MOBA_TOPK = 3
Q_CHUNK = 128
SGU_WIDTH = D_MODEL // 2
N_SGU_GROUPS = 8
SGU_GROUP_DIM = SGU_WIDTH // N_SGU_GROUPS
SGU_CHUNK = 128
D_FF = 4 * D_MODEL
N_MOD = 6
IN_WIDTH = 3 * ATTN_WIDTH + 2 * SGU_WIDTH + 2 * D_MODEL
EPS = 1e-6

kernel_name = "hybrid_moba_gmlp_adaln_block"


def rmsnorm(x, g):
    xf = x.astype(jnp.float32)
    y = xf * lax.rsqrt(jnp.mean(xf * xf, axis=-1, keepdims=True) + EPS)
    return (y * g.astype(jnp.float32)).astype(x.dtype)


def moba_attention(q, k, v):
    B, S, H, Dh = q.shape
    L = MOBA_BLOCK
    nb = -(-S // L)
    s_pad = nb * L
    nqc = S // Q_CHUNK
    k_sel = min(MOBA_TOPK, nb - 1)
    scale = Dh ** -0.5
    qh = q.transpose(0, 2, 1, 3)
    pad = ((0, 0), (0, 0), (0, s_pad - S), (0, 0))
    kb = jnp.pad(k.transpose(0, 2, 1, 3), pad).reshape(B, H, nb, L, Dh)
    vb = jnp.pad(v.transpose(0, 2, 1, 3), pad).reshape(B, H, nb, L, Dh)
    qblk = jnp.arange(S) // L
    if k_sel > 0:
        kmean = jnp.mean(kb.astype(jnp.float32), axis=3)
        gate = jnp.einsum('bhsd,bhnd->bhsn', qh.astype(jnp.float32), kmean)
        fully_past = jnp.arange(nb)[None, :] < qblk[:, None]
        gate = jnp.where(fully_past[None, None], gate, -jnp.inf)
        _, idx = lax.top_k(gate, k_sel)
    else:
        idx = jnp.zeros((B, H, S, 0), jnp.int32)
    valid = idx < qblk[None, None, :, None]

    def to_chunks(a):
        tail = a.shape[3:]
        a = a.reshape((B, H, nqc, Q_CHUNK) + tail).swapaxes(1, 2)
        return a.reshape((B * nqc, H, Q_CHUNK) + tail)

    q_c = to_chunks(qh)
    idx_c = to_chunks(idx)
    valid_c = to_chunks(valid)
    b_ids = jnp.repeat(jnp.arange(B, dtype=jnp.int32), nqc)
    c_ids = jnp.tile(jnp.arange(nqc, dtype=jnp.int32), B)
    heads = jnp.arange(H)[:, None, None]
    offs_q = jnp.arange(Q_CHUNK)
    offs_k = jnp.arange(L)

    def attend_chunk(args):
        qi, ii, vi, bi, ci = args
        kb_b = lax.dynamic_index_in_dim(kb, bi, 0, keepdims=False)
        vb_b = lax.dynamic_index_in_dim(vb, bi, 0, keepdims=False)
        blk = (ci * Q_CHUNK) // L
        k_own = lax.dynamic_index_in_dim(kb_b, blk, 1, keepdims=False)
        v_own = lax.dynamic_index_in_dim(vb_b, blk, 1, keepdims=False)
        qf = qi.astype(jnp.float32) * scale
        s_own = jnp.einsum('hqd,hld->hql', qf, k_own.astype(jnp.float32))
        causal = (blk * L + offs_k)[None, :] <= (ci * Q_CHUNK + offs_q)[:, None]
        s_own = jnp.where(causal[None], s_own, -jnp.inf)
        k_g = kb_b[heads, ii]
        v_g = vb_b[heads, ii]
        s_g = jnp.einsum('hqd,hqkld->hqkl', qf, k_g.astype(jnp.float32))
        s_g = jnp.where(vi[..., None], s_g, -jnp.inf).reshape(H, Q_CHUNK, k_sel * L)
        p = jax.nn.softmax(jnp.concatenate([s_own, s_g], axis=-1), axis=-1)
        p_own = p[..., :L]
        p_g = p[..., L:].reshape(H, Q_CHUNK, k_sel, L)
        o = (jnp.einsum('hql,hld->hqd', p_own, v_own.astype(jnp.float32))
             + jnp.einsum('hqkl,hqkld->hqd', p_g, v_g.astype(jnp.float32)))
        return o.astype(qi.dtype)

    o = lax.map(attend_chunk, (q_c, idx_c, valid_c, b_ids, c_ids))
    o = o.reshape(B, nqc, H, Q_CHUNK, Dh).transpose(0, 1, 3, 2, 4)
    return o.reshape(B, S, H * Dh)


def spatial_gating(u, vs, g_v, w_s, b_s):
    B, S, _ = u.shape
    nc = S // SGU_CHUNK
    vs = rmsnorm(vs, g_v)
    vg = vs.reshape(B, nc, SGU_CHUNK, N_SGU_GROUPS, SGU_GROUP_DIM)
    w_causal = jnp.tril(w_s)
    z = jnp.einsum('gts,bcsgd->bctgd', w_causal, vg) + b_s.T[None, None, :, :, None]
    return u * z.reshape(B, S, SGU_WIDTH)


def setup_inputs(seed: int = 0) -> dict:
    key = jax.random.key(seed)
    ks = jax.random.split(key, 16)
    f32 = jnp.float32
    nrm = lambda k, shape, s: jax.random.normal(k, shape, f32) * s
    return {
        "x": nrm(ks[0], (BATCH, SEQ, D_MODEL), 1.0),
        "c": nrm(ks[1], (BATCH, D_MODEL), 1.0),
        "w_ada": nrm(ks[2], (DEPTH, D_MODEL, N_MOD * D_MODEL), 0.5 * D_MODEL ** -0.5),
        "b_ada": nrm(ks[3], (DEPTH, N_MOD * D_MODEL), 0.01),
        "g_mix": 1.0 + nrm(ks[4], (DEPTH, D_MODEL), 0.01),
        "w_in": nrm(ks[5], (DEPTH, D_MODEL, IN_WIDTH), D_MODEL ** -0.5),
        "w_proj_attn": nrm(ks[6], (DEPTH, ATTN_WIDTH, D_MODEL), ATTN_WIDTH ** -0.5),
        "g_sgu": 1.0 + nrm(ks[7], (DEPTH, SGU_WIDTH), 0.01),
        "w_sgu": nrm(ks[8], (DEPTH, N_SGU_GROUPS, SGU_CHUNK, SGU_CHUNK), SGU_CHUNK ** -0.5),
        "b_sgu": 1.0 + nrm(ks[9], (DEPTH, N_SGU_GROUPS, SGU_CHUNK), 0.01),
        "w_proj_sgu": nrm(ks[10], (DEPTH, SGU_WIDTH, D_MODEL), SGU_WIDTH ** -0.5),
        "w_out": nrm(ks[11], (DEPTH, D_MODEL, D_MODEL), D_MODEL ** -0.5),
        "g_ffn": 1.0 + nrm(ks[12], (DEPTH, D_MODEL), 0.01),
        "w_ff1": nrm(ks[13], (DEPTH, D_MODEL, D_FF), D_MODEL ** -0.5),
        "w_ff2": nrm(ks[14], (DEPTH, D_FF, D_MODEL), D_FF ** -0.5),
        "g_final": 1.0 + nrm(ks[15], (D_MODEL,), 0.01),
    }


def reference(x, c, w_ada, b_ada, g_mix, w_in, w_proj_attn, g_sgu, w_sgu, b_sgu,
              w_proj_sgu, w_out, g_ffn, w_ff1, w_ff2, g_final):
    B, S, _ = x.shape
    A, W, D = ATTN_WIDTH, SGU_WIDTH, D_MODEL
    splits = [A, 2 * A, 3 * A, 3 * A + W, 3 * A + 2 * W, 3 * A + 2 * W + D]
    c_act = jax.nn.silu(c)
    for l in range(DEPTH):
        mod = (c_act @ w_ada[l] + b_ada[l])[:, None, :]
        shift1, scale1, gate1, shift2, scale2, gate2 = jnp.split(mod, N_MOD, axis=-1)
        h = rmsnorm(x, g_mix[l]) * (1 + scale1) + shift1
        proj = h @ w_in[l]
        q, k, v, u, vs, ga, gb = jnp.split(proj, splits, axis=-1)
        hs = (B, S, N_ATTN_HEADS, HEAD_DIM)
        y_attn = moba_attention(q.reshape(hs), k.reshape(hs), v.reshape(hs)) @ w_proj_attn[l]
        y_sgu = spatial_gating(jax.nn.gelu(u), jax.nn.gelu(vs), g_sgu[l], w_sgu[l], b_sgu[l]) @ w_proj_sgu[l]
        merged = jax.nn.sigmoid(ga) * y_attn + jax.nn.sigmoid(gb) * y_sgu
        x = x + gate1 * (merged @ w_out[l])
        h = rmsnorm(x, g_ffn[l]) * (1 + scale2) + shift2
        x = x + gate2 * (jnp.square(jax.nn.relu(h @ w_ff1[l])) @ w_ff2[l])
    return rmsnorm(x, g_final)
```

```python
from contextlib import ExitStack
import numpy as np
import concourse.bass as bass
import concourse.mybir as mybir
from concourse.bass_utils import run_bass_kernel_spmd

F32 = mybir.dt.float32
BF16 = mybir.dt.bfloat16
AF = mybir.ActivationFunctionType
ALU = mybir.AluOpType
AX = mybir.AxisListType

D = 1024
S = 2048
H = 8
DH = 64
T = 512
NJ = 4
NT = S // T
INW = 4608
DFF = 4096
EPS = 1e-6
NEG = -30000.0
NSLOT = 3
NCORES = 8
CONV_BARRIER = False
USE_SEL = True
SEL_APPLY = True


class Res:
    __slots__ = ("name", "w", "r")

    def __init__(self, name):
        self.name = name
        self.w = None
        self.r = {}


class _Rec:
    def __init__(self):
        self.calls = []

    def __getattr__(self, name):
        def f(*a, **k):
            self.calls.append((name, a, k))
            return self
        return f


def _record(f):
    r = _Rec()
    f(r)
    assert len(r.calls) == 1, r.calls
    return r.calls[0]


class Sched:
    ENG = ("pe", "act", "dve", "pool", "sp")

    def __init__(self):
        self.prog = {e: [] for e in self.ENG}
        self.cnt = {}
        self.waited = {e: {} for e in self.ENG}

    def op(self, eng, fns, reads=(), writes=(), sem=None, inc=1):
        if not isinstance(fns, (list, tuple)):
            fns = [fns]
        own = "s_" + eng
        semk = sem or own
        need = {}

        def add(ev, same_ok):
            if ev is None:
                return
            k, v = ev
            if k == own and not same_ok:
                return
            if need.get(k, 0) < v:
                need[k] = v

        raw_same = eng in ("act", "dve", "pool")
        for r in reads:
            add(r.w, raw_same)
        for w in writes:
            add(w.w, raw_same)
            for k, v in w.r.items():
                add((k, v), raw_same)
        wd = self.waited[eng]
        for k, v in need.items():
            if wd.get(k, 0) < v:
                wd[k] = v
                self.prog[eng].append(("w", k, v))
        self.cnt[semk] = self.cnt.get(semk, 0) + inc
        ev = (semk, self.cnt[semk])
        for f in fns[:-1]:
            self.prog[eng].append(("i", _record(f), None, 0))
        self.prog[eng].append(("i", _record(fns[-1]), semk, inc))
        for r in reads:
            if r.r.get(semk, 0) < ev[1]:
                r.r[semk] = ev[1]
        for w in writes:
            w.w = ev
            w.r = {}
        return ev

    def wait_final(self, eng, semk):
        v = self.cnt.get(semk, 0)
        if v and self.waited[eng].get(semk, 0) < v:
            self.waited[eng][semk] = v
            self.prog[eng].append(("w", semk, v))


class _Stop(Exception):
    pass


def build_nc(NB, stop=None):
    nc = bass.Bass("TRN2", target_bir_lowering=False)

    def ck(n):
        if stop == n:
            raise _Stop()
    sc = Sched()
    es = ExitStack()

    def din(name, shape, dt=F32):
        return nc.dram_tensor(name, list(shape), dt, kind="ExternalInput").ap()

    x_d = din("x", [NB, S, D])
    c_d = din("c", [NB, D])
    wada_d = din("w_ada", [D, 6 * D])
    bada_d = din("b_ada", [1, 6 * D])
    gmix_d = din("g_mix", [1, D])
    win_d = din("w_in", [D, INW])
    wpa_d = din("w_proj_attn", [512, D])
    gsgu_d = din("g_sgu", [1, 512])
    wsgu_d = din("w_sgu", [8, 128, 128])
    bsgu_d = din("b_sgu", [8, 128])
    wps_d = din("w_proj_sgu", [512, D])
    wout_d = din("w_out", [D, D])
    gffn_d = din("g_ffn", [1, D])
    wff1_d = din("w_ff1", [D, DFF])
    wff2_d = din("w_ff2", [DFF, D])
    gfin_d = din("g_final", [1, D])
    identf_d = din("c_ident", [128, 128])
    tri_d = din("c_tri", [128, 128])
    maskT_d = din("c_maskT", [128, 128])
    eoh_d = din("c_eoh", [8, 8 * 128])
    out_d = nc.dram_tensor("out", [NB, S, D], F32, kind="ExternalOutput").ap()

    NCH = 31
    wsc = nc.dram_tensor("wsc", [NCH, 128, 4096], BF16, kind="Internal").ap()
    wsc_res = [Res(f"wsc{i}") for i in range(NCH)]

    def sb(name, shape, dt):
        return es.enter_context(nc.sbuf_tensor(name, list(shape), dt))

    xsb = [sb(f"xs{i}", [128, NJ, D], F32) for i in range(2)]
    xs_res = [Res(f"xs{i}") for i in range(2)]
    junk = sb("junk", [128, D], BF16)
    xn = sb("xn", [128, NJ, D], BF16)
    hT = sb("hT", [128, 8, T], BF16)
    h2T = sb("h2T", [128, 8, T], BF16)
    Qp = sb("Qp", [128, 8, T], BF16)
    Kc = sb("Kc", [128, 4, S], BF16)
    Vc = sb("Vc", [128, 16, 8, 65], BF16)
    ones64 = sb("ones64", [1, 64], F32)
    arena = sb("arena", [128, 8192], F32)
    arena_b = arena[:].bitcast(BF16)
    ares = [Res(f"ar{i}") for i in range(32)]
    rec = [sb(f"rec{i}", [64, 512], F32) for i in range(2)]
    tha = sb("tha", [128, T], F32)
    thb = sb("thb", [128, T], F32)
    tmp = [sb(f"tmp{i}", [128, 512], F32) for i in range(2)]
    sq = [sb(f"sq{i}", [128, T], F32) for i in range(2)]
    wsl = sb("wsl", [128, NSLOT * 2048], F32)
    identf = sb("identf", [128, 128], F32)
    identb = sb("identb", [128, 128], BF16)
    tri = sb("tri", [128, 128], BF16)
    maskT = sb("maskT", [128, 128], F32)
    eoh = sb("eoh", [128, 8, 128], BF16)
    wcT = sb("wcT", [128, 8, 128], BF16)
    bT = sb("bT", [128, 4, 128], F32)
    gsgu_bc = sb("gsgu_bc", [128, 512], F32)
    gfin_bc = sb("gfin_bc", [128, D], F32)
    g1h_bc = sb("g1h_bc", [128, D], F32)
    g2_bc = sb("g2_bc", [128, D], F32)
    gsc = nc.dram_tensor("gsc", [4, 2, D], F32, kind="Internal").ap()
    modc = [sb(f"modc{i}", [4, 256], F32) for i in range(2)]
    modT = sb("modT", [128, 48, 4], F32)
    gmT = sb("gmT", [128, 8], F32)
    gfT = sb("gfT", [128, 8], F32)
    A1 = sb("A1", [128, 8, 4], F32)
    A2 = sb("A2", [128, 8, 4], F32)
    cactT = sb("cactT", [128, 8, 4], F32)
    ssq = sb("ssq", [128, 16], F32)
    ksum = sb("ksum", [128, 4, 8], F32)
    ksb = sb("ksb", [128, 4, 8], BF16)
    Gs = sb("Gs", [128, 8, 8], F32)
    cmpt = sb("cmpt", [128, 8, 49], F32)
    rank = sb("rank", [128, 8, 8], F32)
    selb = sb("selb", [128, 8, 8], BF16)
    selT = sb("selT", [128, 8, T], BF16)

    pbank = [es.enter_context(nc.psum_tensor(f"pb{i}", [128, 512], F32)) for i in range(8)]
    pres = [Res(f"pb{i}") for i in range(8)]
    prr = [0, 0]

    def pget(grp):
        i = grp * 4 + prr[grp] % 4
        prr[grp] += 1
        return pbank[i], pres[i]

    R = {n: Res(n) for n in (
        "junk", "xn", "hT", "h2T", "Qp", "Vc", "tha", "thb", "identf", "identb", "tri", "maskT", "eoh", "gsc",
        "wcT", "bT", "gsgu_bc", "gfin_bc", "g1h_bc", "g2_bc", "modT", "gmT", "gfT", "A1", "A2",
        "cactT", "ones64", "ssq", "ksum", "ksb", "Gs", "cmpt", "rank", "selb", "selT", "out")}
    Kc_res = [Res(f"Kc{i}") for i in range(NT)]
    mT_res = [Res(f"mT{i}") for i in range(8)]
    tmp_res = [Res(f"tmp{i}") for i in range(2)]
    sq_res = [Res(f"sq{i}") for i in range(2)]
    rec_res = [Res(f"rec{i}") for i in range(2)]
    modc_res = [Res(f"modc{i}") for i in range(2)]
    wres = [Res(f"wsl{i}") for i in range(NSLOT)]
    wrr = [0]

    def aT(ct):
        return arena_b[:, ct * 512:(ct + 1) * 512], [ares[ct]]

    arena_f = arena[:]
    bada_sb = arena_f[0:4, 1024:7168]
    bada_res = ares[4:28]
    cin = arena_f[0:4, 7168:8192]
    cact = cin
    cin_res = ares[28:32]

    def uT(ct):
        return arena_f[:, ct * 512:(ct + 1) * 512], [ares[2 * ct], ares[2 * ct + 1]]

    def vsg(j):
        return arena_f[:, 2048 + j * 512:2048 + (j + 1) * 512], [ares[8 + 2 * j], ares[9 + 2 * j]]

    def vn(j):
        return arena_b[:, 8192 + j * 512:8192 + (j + 1) * 512], [ares[16 + j]]

    def mT(ct):
        return arena_b[:, 4096 + ct * 512:4096 + (ct + 1) * 512], [ares[8 + ct]]

    def sT(gp):
        return arena_b[:, 10240 + gp * 512:10240 + (gp + 1) * 512], [ares[20 + gp]]

    def oT(pair):
        return arena_b[:, 12288 + pair * 512:12288 + (pair + 1) * 512], [ares[24 + pair]]

    def PT(i):
        return arena_b[:, 14336 + i * 512:14336 + (i + 1) * 512], [ares[28 + i]]

    def wslot_b(s):
        return wsl[:, s * 2048:(s + 1) * 2048].bitcast(BF16)

    dma_n = {"sp": 0, "pool": 0}
    NLANE = {"sp": 8, "pool": 24}
    lane_res = {e: [Res(f"{e}_lane{i}") for i in range(NLANE[e])] for e in NLANE}

    def dma(eng, out, in_, reads, writes, sem=None, ncdma=False, eng_override=None):
        if eng_override is not None:
            eng = eng_override
        if sem is None:
            lane = dma_n[eng] % NLANE[eng]
            dma_n[eng] += 1
            sem = f"{eng}_l{lane}"
            writes = list(writes) + [lane_res[eng][lane]]
        if ncdma:
            f = lambda e, o=out, i=in_: e.dma_start(out=o, in_=i, allow_slow_non_contiguous=True)
        else:
            f = lambda e, o=out, i=in_: e.dma_start(out=o, in_=i)
        return sc.op(eng, f, reads, writes, sem=sem, inc=16)

    try:
        dma("sp", identf[:], identf_d, [], [R["identf"]])
        dma("pool", identb[:], identf_d, [], [R["identb"]])
        dma("pool", tri[:], tri_d, [], [R["tri"]])
        dma("sp", maskT[:], maskT_d, [], [R["maskT"]])
        sc.op("dve", lambda e: e.memset(ones64[:], 1.0), [], [R["ones64"]])
        sc.op("dve", lambda e: e.memset(Vc[:, :, :, 64:65], 1.0), [], [R["Vc"]])
        sc.op("dve", lambda e: e.memset(eoh[:], 0.0), [], [R["eoh"]])
        sc.op("dve", lambda e: e.memset(selT[:], 0.0), [], [R["selT"]])
        sc.op("dve", lambda e: e.memset(Qp[:], 0.0), [], [R["Qp"]])
        dma("pool", eoh[0:8], eoh_d.rearrange("j (k n) -> j k n", k=8), [], [R["eoh"]])
        dma("pool", eoh[64:72], eoh_d.rearrange("j (k n) -> j k n", k=8), [], [R["eoh"]])
        dma("sp", cin[0:NB, :], c_d, [], cin_res)
        dma("sp", bada_sb, bada_d[0:1, :].to_broadcast([4, 6 * D]), [], bada_res)
        dma("sp", gsgu_bc[:], gsgu_d[0:1, :].to_broadcast([128, 512]), [], [R["gsgu_bc"]])
        dma("sp", gfin_bc[:], gfin_d[0:1, :].to_broadcast([128, D]), [], [R["gfin_bc"]])
        dma("sp", gmT[:], gmix_d[0, :].rearrange("(k p) -> p k", p=128), [], [R["gmT"]], ncdma=True)
        dma("sp", gfT[:], gffn_d[0, :].rearrange("(k p) -> p k", p=128), [], [R["gfT"]], ncdma=True)
        for g in range(8):
            dma("sp", bT[(g % 2) * 64:(g % 2) * 64 + 64, g // 2, :], bsgu_d[g:g + 1, :].to_broadcast([64, 128]), [], [R["bT"]])

        def conv(ci, pieces):
            evs = []
            for (c0, ncols, src, nkt) in pieces:
                o = wsc[ci, :, c0:c0 + nkt * ncols].rearrange("p (k n) -> p k n", k=nkt)
                i = src.rearrange("(k p) n -> p k n", p=128)
                evs.append((o, i))
            for n, (o, i) in enumerate(evs):
                dma("pool", o, i, [], [wsc_res[ci]] if n == len(evs) - 1 else [])

        def conv_multi(ci, pieces):
            subs = []
            for n, (c0, ncols, src, nkt) in enumerate(pieces):
                o = wsc[ci, :, c0:c0 + nkt * ncols].rearrange("p (k n) -> p k n", k=nkt)
                i = src.rearrange("(k p) n -> p k n", p=128)
                r = Res(f"wsc{ci}_{n}")
                dma("pool", o, i, [R["modT"]] if ci >= 5 else [], [r])
                subs.append(r)
            return subs

        wsc_sub = {}
        for q in range(5):
            wsc_sub[q] = conv_multi(q, [(0, 512, win_d[:, q * 512:(q + 1) * 512], 8)])
        ck(1)

        sc.op("act", lambda e: e.activation(out=cact[0:NB, :], in_=cin[0:NB, :], func=AF.Silu), cin_res, cin_res)
        pb, pr = pget(0)
        sc.op("pe", [(lambda e, k=k: e.transpose(out=pb[:, k * 4:k * 4 + NB], in_=cact[0:NB, k * 128:(k + 1) * 128],
                                                 identity=identf[0:NB, 0:NB])) for k in range(8)],
              cin_res + [R["identf"]], [pr])
        sc.op("dve", lambda e: e.memset(cactT[:], 0.0), [], [R["cactT"]])
        sc.op("dve", lambda e: e.tensor_copy(out=cactT[:, :, 0:NB], in_=pb[:, 0:32].rearrange("p (k b) -> p k b", b=4)[:, :, 0:NB]),
              [pr], [R["cactT"]])

        NCC = 24
        for cc in range(NCC):
            s = wrr[0] % NSLOT
            wrr[0] += 1
            wv = wsl[:, s * 2048:(s + 1) * 2048].rearrange("p (k n) -> p k n", k=8)
            dma("sp", wv, wada_d[:, cc * 256:(cc + 1) * 256].rearrange("(k p) n -> p k n", p=128), [], [wres[s]], sem=f"w{s}")
            pb, pr = pget(0)
            sc.op("pe", [(lambda e, k=k, wv=wv, pb=pb: e.matmul(pb[0:4, 0:256], lhsT=cactT[:, k, :], rhs=wv[:, k, :],
                                                              start=(k == 0), stop=(k == 7))) for k in range(8)],
                  [wres[s], R["cactT"]], [pr])
            mi = cc % 2
            sc.op("dve", lambda e, pb=pb, mi=mi, cc=cc: e.tensor_tensor(out=modc[mi][:, 0:256], in0=pb[0:4, 0:256],
                                                                        in1=bada_sb[:, cc * 256:(cc + 1) * 256], op=ALU.add),
                  [pr] + bada_res, [modc_res[mi]])
            col = cc * 256
            if 2048 <= col < 3072 or 5120 <= col < 6144:
                gi = 0 if col < 3072 else 1
                off = col - (2048 if gi == 0 else 5120)
                dma("sp", gsc[:, gi, off:off + 256], modc[mi][:, 0:256], [modc_res[mi]], [R["gsc"]])
            pb2, pr2 = pget(1)
            sc.op("pe", [(lambda e, q=q, mi=mi, pb2=pb2: e.transpose(out=pb2[:, q * 4:q * 4 + 4], in_=modc[mi][:, q * 128:(q + 1) * 128],
                                                                    identity=identf[0:4, 0:4])) for q in range(2)],
                  [modc_res[mi], R["identf"]], [pr2])
            sc.op("act", lambda e, cc=cc, pb2=pb2: e.activation(out=modT[:, cc * 2:cc * 2 + 2, :],
                                                               in_=pb2[:, 0:8].rearrange("p (q b) -> p q b", b=4), func=AF.Copy),
                  [pr2], [R["modT"]])
        ck(2)

        for ct in range(8):
            wsc_sub[5 + ct] = conv_multi(5 + ct, [
                (0, 128, win_d[:, 2560 + ct * 128:2560 + (ct + 1) * 128], 8),
                (1024, 128, win_d[:, 3584 + ct * 128:3584 + (ct + 1) * 128], 8),
                (2048, 128, wpa_d[:, ct * 128:(ct + 1) * 128], 4),
                (2560, 128, wps_d[:, ct * 128:(ct + 1) * 128], 4)])
        for dh in range(2):
            wsc_sub[13 + dh] = conv_multi(13 + dh, [(0, 512, wout_d[:, dh * 512:(dh + 1) * 512], 8)])
        for c8 in range(8):
            wsc_sub[15 + c8] = conv_multi(15 + c8, [(0, 512, wff1_d[:, c8 * 512:(c8 + 1) * 512], 8)])
        for dh in range(2):
            for kc in range(4):
                wsc_sub[23 + dh * 4 + kc] = conv_multi(23 + dh * 4 + kc,
                                                       [(0, 512, wff2_d[kc * 1024:(kc + 1) * 1024, dh * 512:(dh + 1) * 512], 8)])

        sc.op("dve", lambda e: e.scalar_tensor_tensor(out=A1[:], in0=modT[:, 8:16, :], scalar=1.0,
                                                      in1=gmT[:].unsqueeze(2).to_broadcast([128, 8, 4]), op0=ALU.add, op1=ALU.mult),
              [R["modT"], R["gmT"]], [R["A1"]])
        sc.op("dve", lambda e: e.scalar_tensor_tensor(out=A2[:], in0=modT[:, 32:40, :], scalar=1.0,
                                                      in1=gfT[:].unsqueeze(2).to_broadcast([128, 8, 4]), op0=ALU.add, op1=ALU.mult),
              [R["modT"], R["gfT"]], [R["A2"]])

        wtmp = arena_f[:, 0:1024].rearrange("p (g s) -> p g s", g=8)
        wtmp_res = [ares[0], ares[1], ares[2], ares[3]]
        dma("sp", wtmp, wsgu_d.rearrange("g t s -> t g s"), [], wtmp_res)
        for half in range(2):
            pb, pr = pget(0)
            sc.op("pe", [(lambda e, g=g, pb=pb, half=half: e.transpose(out=pb[:, g * 128:(g + 1) * 128], in_=wtmp[:, half * 4 + g, :],
                                                                      identity=identf[:])) for g in range(4)],
                  wtmp_res + [R["identf"]], [pr])
            sc.op("dve", lambda e, pb=pb, half=half: e.tensor_tensor(
                out=wcT[:, half * 4:half * 4 + 4, :], in0=pb[:, 0:512].rearrange("p (g t) -> p g t", g=4),
                in1=maskT[:].unsqueeze(1).to_broadcast([128, 4, 128]), op=ALU.mult),
                [pr, R["maskT"]], [R["wcT"]])

        sc.op("dve", lambda e: e.memset(ksum[:], 0.0), [], [R["ksum"]])
        sc.op("dve", lambda e: e.memset(ksb[:], 0.0), [], [R["ksb"]])
        ck(3)

        def load_w(ci, n=4096):
            s = wrr[0] % NSLOT
            wrr[0] += 1
            dma("sp", wslot_b(s)[:, 0:n], wsc[ci, :, 0:n], wsc_sub[ci], [wres[s]], sem=f"w{s}")
            return wslot_b(s), wres[s]

        def stats_begin(col0, n=NJ):
            sc.op("dve", lambda e: e.memset(ssq[:, col0:col0 + n], 0.0), [], [R["ssq"]])

        def stats_row(col0, j, src, src_res, width):
            sc.op("act", lambda e: e.activation(out=junk[:, 0:width], in_=src, func=AF.Square,
                                                accum_out=ssq[:, col0 + j:col0 + j + 1]),
                  list(src_res) + [R["ssq"]], [R["junk"], R["ssq"]])

        def rms_stats(col0, src_fn, src_res, width, n=NJ, rows=True):
            if rows:
                stats_begin(col0, n)
                for j in range(n):
                    stats_row(col0, j, src_fn(j), src_res(j), width)
            rstd_cols(col0, n, width)

        def rstd_cols(col0, n, width):
            sc.op("dve", lambda e: e.tensor_scalar(out=ssq[:, col0:col0 + n], in0=ssq[:, col0:col0 + n], scalar1=1.0 / width,
                                                   scalar2=EPS, op0=ALU.mult, op1=ALU.add), [R["ssq"]], [R["ssq"]])
            sc.op("act", lambda e: e.activation(out=ssq[:, col0:col0 + n], in_=ssq[:, col0:col0 + n], func=AF.Sqrt),
                  [R["ssq"]], [R["ssq"]])
            sc.op("dve", lambda e: e.reciprocal(out=ssq[:, col0:col0 + n], in_=ssq[:, col0:col0 + n]), [R["ssq"]], [R["ssq"]])

        def norm_xn(col0, xs, Rxs, rows=None):
            for j in (range(NJ) if rows is None else rows):
                sc.op("dve", lambda e, j=j: e.tensor_scalar(out=xn[:, j, :], in0=xs[:, j, :], scalar1=ssq[:, col0 + j:col0 + j + 1],
                                                            scalar2=None, op0=ALU.mult), [Rxs, R["ssq"]], [R["xn"]])

        def norm_to_T(col0, dstT, dst_res, Amod, shcol, b, xs, Rxs, do_xn=True):
            if do_xn:
                norm_xn(col0, xs, Rxs)
            for k in range(8):
                pb, pr = pget(0)
                pbb = pb[:].bitcast(BF16)
                sc.op("pe", [(lambda e, j=j, k=k, pbb=pbb: e.transpose(out=pbb[:, j * 128:(j + 1) * 128],
                                                                      in_=xn[:, j, k * 128:(k + 1) * 128], identity=identb[:]))
                             for j in range(NJ)], [R["xn"], R["identb"]], [pr])
                sc.op("act", lambda e, k=k, pbb=pbb: e.activation(out=dstT[:, k, :], in_=pbb[:, 0:T], func=AF.Identity,
                                                                 scale=Amod[:, k, b:b + 1], bias=modT[:, shcol + k, b:b + 1]),
                      [pr, R["A1"], R["A2"], R["modT"]], [dst_res])

        def mm_feat(w, wr, ct, src, src_res):
            pb, pr = pget(0)
            sc.op("pe", [(lambda e, k=k, pb=pb: e.matmul(pb[:, :], lhsT=w[:, k, ct * 128:(ct + 1) * 128], rhs=src[:, k, :],
                                                       start=(k == 0), stop=(k == 7))) for k in range(8)],
                  [wr, src_res], [pr])
            return pb, pr

        def mm_tok(w, wr, j, src, src_res, grp=0):
            pb, pr = pget(grp)
            sc.op("pe", [(lambda e, k=k, pb=pb: e.matmul(pb[:, :], lhsT=src[:, k, j * 128:(j + 1) * 128], rhs=w[:, k, :],
                                                       start=(k == 0), stop=(k == 7))) for k in range(8)],
                  [wr, src_res], [pr])
            return pb, pr

        tiles = [(bb, tt) for bb in range(NB) for tt in range(NT)]

        def prefetch_A(idx):
            bb, tt = tiles[idx]
            p = idx % 2
            xs_, Rx = xsb[p], xs_res[p]
            dma("sp", xs_[:], x_d[bb, tt * T:(tt + 1) * T, :].rearrange("(j p) d -> p j d", p=128), [], [Rx], sem=f"x_ld{p}")
            rms_stats(0, lambda j: xs_[:, j, :], lambda j: [Rx], D)
            norm_xn(0, xs_, Rx)

        def prefetch_B(idx):
            bb, tt = tiles[idx]
            p = idx % 2
            norm_to_T(0, hT, R["hT"], A1, 0, bb, xsb[p], xs_res[p], do_xn=False)

        prefetch_A(0)
        prefetch_B(0)
        for idx, (b, t) in enumerate(tiles):
            if True:
                tok0 = t * T
                xs, Rxs = xsb[idx % 2], xs_res[idx % 2]
                if t == 0:
                    gq = "sp" if b == 0 else "pool"
                    dma(gq, g1h_bc[:], gsc[b:b + 1, 0, :].to_broadcast([128, D]), [R["gsc"]], [R["g1h_bc"]])
                    dma(gq, g2_bc[:], gsc[b:b + 1, 1, :].to_broadcast([128, D]), [R["gsc"]], [R["g2_bc"]])
                ck(t * 10 + 4)

                w, wr = load_w(4)
                w = w.rearrange("p (k n) -> p k n", k=8)
                for j in range(NJ):
                    pb, pr = mm_tok(w, wr, j, hT, R["hT"])
                    va, vr = vsg(j)
                    sc.op("act", lambda e, pb=pb, va=va: e.activation(out=va, in_=pb[:, :], func=AF.Gelu_apprx_tanh), [pr], vr)
                rms_stats(4, lambda j: vsg(j)[0], lambda j: vsg(j)[1], 512)
                for j in range(NJ):
                    va, vr = vsg(j)
                    na, nr = vn(j)
                    sc.op("dve", lambda e, va=va, na=na, j=j: e.scalar_tensor_tensor(out=na, in0=va, scalar=ssq[:, 4 + j:5 + j],
                                                                                    in1=gsgu_bc[:], op0=ALU.mult, op1=ALU.mult),
                          vr + [R["ssq"], R["gsgu_bc"]], nr)
                w, wr = load_w(3)
                w = w.rearrange("p (k n) -> p k n", k=8)
                for ct in range(4):
                    pb, pr = mm_feat(w, wr, ct, hT, R["hT"])
                    ua, ur = uT(ct)
                    sc.op("act", lambda e, pb=pb, ua=ua: e.activation(out=ua, in_=pb[:, :], func=AF.Gelu_apprx_tanh), [pr], ur)
                w, wr = load_w(0)
                w = w.rearrange("p (k n) -> p k n", k=8)
                for ct in range(4):
                    pb, pr = mm_feat(w, wr, ct, hT, R["hT"])
                    for par in range(2):
                        sc.op("dve", lambda e, pb=pb, ct=ct, par=par: e.tensor_scalar(
                            out=Qp[par * 64:(par + 1) * 64, 2 * ct + par, :], in0=pb[par * 64:(par + 1) * 64, :], scalar1=0.125,
                            scalar2=None, op0=ALU.mult), [pr], [R["Qp"]])
                w, wr = load_w(1)
                w = w.rearrange("p (k n) -> p k n", k=8)
                sc.op("dve", lambda e: e.memset(ksum[:, :, 2 * t:2 * t + 2], 0.0), [], [R["ksum"]])
                for ct in range(4):
                    pb, pr = mm_feat(w, wr, ct, hT, R["hT"])
                    sc.op("act", [(lambda e, pb=pb, ct=ct, bl=bl: e.activation(
                        out=Kc[:, ct, tok0 + bl * 256:tok0 + (bl + 1) * 256], in_=pb[:, bl * 256:(bl + 1) * 256], func=AF.Copy,
                        accum_out=ksum[:, ct, 2 * t + bl:2 * t + bl + 1])) for bl in range(2)],
                          [pr, R["ksum"]], [Kc_res[t], R["ksum"]])
                sc.op("dve", lambda e: e.tensor_copy(out=ksb[:, :, 2 * t:2 * t + 2], in_=ksum[:, :, 2 * t:2 * t + 2]),
                      [R["ksum"]], [R["ksb"]])
                ck(t * 10 + 5)
                use_sel = (t >= 2) and USE_SEL
                if use_sel:
                    for j in range(NJ):
                        qb = 2 * t + j // 2
                        pbe = [pget(1), pget(1)]
                        for par in range(2):
                            pb, pr = pbe[par]
                            sc.op("pe", [(lambda e, pb=pb, h=h, j=j, qb=qb: e.matmul(
                                pb[:, (h // 2) * 8:(h // 2) * 8 + qb], lhsT=Qp[:, h, j * 128:(j + 1) * 128],
                                rhs=ksb[:, h // 2, 0:qb], start=True, stop=True)) for h in range(par, H, 2)],
                                  [R["Qp"], R["ksb"]], [pr])
                        for par in range(2):
                            pb, pr = pbe[par]
                            sc.op("dve", lambda e, pb=pb, qb=qb, par=par: e.tensor_copy(
                                out=Gs[:, par::2, 0:qb], in_=pb[:, 0:32].rearrange("p (h n) -> p h n", n=8)[:, :, 0:qb]),
                                  [pr], [R["Gs"]])
                        sc.op("dve", lambda e, qb=qb: e.tensor_tensor(
                            out=cmpt[:, :, 0:qb * qb].rearrange("p h (a m) -> p h a m", m=qb),
                            in0=Gs[:, :, 0:qb].unsqueeze(2).to_broadcast([128, 8, qb, qb]),
                            in1=Gs[:, :, 0:qb].unsqueeze(3).to_broadcast([128, 8, qb, qb]), op=ALU.is_gt),
                            [R["Gs"]], [R["cmpt"]])
                        sc.op("dve", lambda e, qb=qb: e.reduce_sum(out=rank[:, :, 0:qb],
                                                                  in_=cmpt[:, :, 0:qb * qb].rearrange("p h (a m) -> p h a m", m=qb),
                                                                  axis=AX.X), [R["cmpt"]], [R["rank"]])
                        sc.op("dve", lambda e: e.memset(selb[:], 0.0), [], [R["selb"]])
                        sc.op("dve", lambda e, qb=qb: e.tensor_scalar(out=selb[:, :, 0:qb], in0=rank[:, :, 0:qb], scalar1=2.5, scalar2=NEG,
                                                                     op0=ALU.is_ge, op1=ALU.mult), [R["rank"], R["selb"]], [R["selb"]])
                        pb2, pr2 = pget(1)
                        sc.op("pe", [(lambda e, h=h, pb2=pb2: e.matmul(
                            pb2[(h % 2) * 64:(h % 2) * 64 + 8, (h // 2) * 128:(h // 2 + 1) * 128], lhsT=selb[:, h, :], rhs=identb[:],
                            start=True, stop=True, tile_position=(0, (h % 2) * 64))) for h in range(H)],
                              [R["selb"], R["identb"]], [pr2])
                        for par in range(2):
                            sc.op("dve", lambda e, j=j, pb2=pb2, par=par: e.tensor_copy(
                                out=selT[par * 64:par * 64 + 8, par::2, j * 128:(j + 1) * 128],
                                in_=pb2[par * 64:par * 64 + 8, :].rearrange("p (h q) -> p h q", h=4)),
                                  [pr2], [R["selT"]])

                w, wr = load_w(2)
                w = w.rearrange("p (k n) -> p k n", k=8)
                for j in range(NJ):
                    pb, pr = mm_tok(w, wr, j, hT, R["hT"])
                    sc.op("dve", lambda e, pb=pb, j=j: e.tensor_copy(out=Vc[:, 4 * t + j, :, 0:64],
                                                                    in_=pb[:, :].rearrange("p (h d) -> p h d", d=64)),
                          [pr], [R["Vc"]])
                for gp in range(4):
                    pb, pr = pget(1)
                    fns = []
                    rds = [R["wcT"]]
                    for j in range(NJ):
                        na, nr = vn(j)
                        rds += nr
                        for hf in range(2):
                            g = 2 * gp + hf
                            fns.append(lambda e, pb=pb, na=na, j=j, hf=hf, g=g: e.matmul(
                                pb[hf * 64:(hf + 1) * 64, j * 128:(j + 1) * 128], lhsT=na[:, g * 64:(g + 1) * 64], rhs=wcT[:, g, :],
                                start=True, stop=True, tile_position=(0, hf * 64)))
                    sc.op("pe", fns, rds, [pr])
                    sa, sr = sT(gp)
                    ua, ur = uT(gp)
                    t0, t0r = tmp[gp % 2], tmp_res[gp % 2]
                    sc.op("dve", lambda e, pb=pb, t0=t0, gp=gp: e.tensor_tensor(
                        out=t0[:, :].rearrange("p (j t) -> p j t", t=128), in0=pb[:, :].rearrange("p (j t) -> p j t", t=128),
                        in1=bT[:, gp:gp + 1, :].to_broadcast([128, 4, 128]), op=ALU.add), [pr, R["bT"]], [t0r])
                    sc.op("dve", lambda e, t0=t0, sa=sa, ua=ua: e.tensor_tensor(out=sa, in0=t0[:, :], in1=ua, op=ALU.mult),
                          [t0r] + ur, sr)

                steps = []
                for h in range(H):
                    ents = []
                    for kt in range(4 * t):
                        ents.append((kt, 0, 512, None, (0, 512) if use_sel else None))
                    ents.append((4 * t, 0, 512, (0, 128), (256, 512) if use_sel else None))
                    ents.append((4 * t + 1, 128, 512, (128, 256), (256, 512) if use_sel else None))
                    ents.append((4 * t + 2, 256, 512, (256, 384), None))
                    ents.append((4 * t + 3, 384, 512, (384, 512), None))
                    for i, en in enumerate(ents):
                        steps.append((h, i, len(ents), en))
                acc_of = {}
                pending_fin = []

                def emit_scores(si):
                    h, i, n, (kt, c0, c1, tric, biasc) = steps[si]
                    off, pair = (h % 2) * 64, h // 2
                    pb, pr = pget(0)
                    fns = []
                    rds = [R["Qp"], Kc_res[kt // 4]]
                    extra = (tric is not None) or (biasc is not None and SEL_APPLY)
                    nextra = (1 if tric is not None else 0) + (1 if (biasc is not None and SEL_APPLY) else 0)
                    fns.append(lambda e: e.matmul(pb[:, c0:c1], lhsT=Kc[:, pair, kt * 128:(kt + 1) * 128],
                                                  rhs=Qp[:, h, c0:c1], start=True, stop=not extra))
                    if biasc is not None and SEL_APPLY:
                        nextra -= 1
                        rds += [R["eoh"], R["selT"]]
                        fns.append(lambda e, last=(nextra == 0): e.matmul(
                            pb[:, biasc[0]:biasc[1]], lhsT=eoh[:, kt // 2, :],
                            rhs=selT[:, h, biasc[0]:biasc[1]], start=False, stop=last))
                    if tric is not None:
                        rds += [R["identb"], R["tri"]]
                        fns.append(lambda e: e.matmul(pb[:, tric[0]:tric[1]], lhsT=identb[:], rhs=tri[:], start=False, stop=True))
                    sc.op("pe", fns, rds, [pr])
                    return pb, pr

                def emit_exp_pv(si, pb, pr, pslot):
                    h, i, n, (kt, c0, c1, tric, biasc) = steps[si]
                    off, pair = (h % 2) * 64, h // 2
                    pa, par = PT(pslot)
                    sc.op("act", lambda e: e.activation(out=pa[:, c0:c1], in_=pb[:, c0:c1], func=AF.Exp), [pr], par)
                    if i == 0:
                        acc_of[h] = pget(1)
                    ab, ar = acc_of[h]
                    first, last = (i == 0), (i == n - 1)
                    sc.op("pe", lambda e: e.matmul(ab[0:65, c0:c1], lhsT=Vc[:, kt, h, :], rhs=pa[:, c0:c1], start=first, stop=last),
                          par + [R["Vc"]], [ar])
                    if last:
                        ri = h % 2
                        sc.op("dve", lambda e: e.reciprocal(out=rec[ri][0:1, :], in_=ab[64:65, :]), [ar], [rec_res[ri]])

                        def finalize():
                            zb, zr = pget(1)
                            sc.op("pe", lambda e: e.matmul(zb[0:64, :], lhsT=ones64[0:1, :], rhs=rec[ri][0:1, :], start=True, stop=True),
                                  [rec_res[ri], R["ones64"]], [zr])
                            sc.op("act", lambda e: e.activation(out=tmp[ri][0:64, :], in_=zb[0:64, :], func=AF.Copy), [zr], [tmp_res[ri]])
                            oa, orr = oT(pair)
                            sc.op("dve", lambda e: e.tensor_tensor(out=oa[off:off + 64, :], in0=ab[0:64, :], in1=tmp[ri][0:64, :],
                                                                   op=ALU.mult), [ar, tmp_res[ri]], orr)
                        pending_fin.append(finalize)

                nst = len(steps)
                LOOK = 1
                pend = [emit_scores(si) for si in range(min(LOOK, nst))]
                for si in range(nst):
                    if si + LOOK < nst:
                        pend.append(emit_scores(si + LOOK))
                    pb_, pr_ = pend.pop(0)
                    emit_exp_pv(si, pb_, pr_, si % 4)
                    if pending_fin and (steps[si][1] == 2 or si == nst - 1):
                        while pending_fin:
                            pending_fin.pop(0)()

                ck(t * 10 + 6)
                for ct in range(8):
                    w, wr = load_w(5 + ct, 3072)
                    wga = w[:, 0:1024].rearrange("p (k n) -> p k n", k=8)
                    wgb = w[:, 1024:2048].rearrange("p (k n) -> p k n", k=8)
                    wpa = w[:, 2048:2560].rearrange("p (k n) -> p k n", k=4)
                    wps = w[:, 2560:3072].rearrange("p (k n) -> p k n", k=4)
                    grp = ct % 2
                    pC, pCr = pget(grp)
                    sc.op("pe", [(lambda e, k=k, pC=pC, wga=wga: e.matmul(pC[:, :], lhsT=wga[:, k, :], rhs=hT[:, k, :], start=(k == 0), stop=(k == 7)))
                                 for k in range(8)], [wr, R["hT"]], [pCr])
                    pD, pDr = pget(grp)
                    sc.op("pe", [(lambda e, k=k, pD=pD, wgb=wgb: e.matmul(pD[:, :], lhsT=wgb[:, k, :], rhs=hT[:, k, :], start=(k == 0), stop=(k == 7)))
                                 for k in range(8)], [wr, R["hT"]], [pDr])
                    pA, pAr = pget(grp)
                    rds = [wr]
                    for k in range(4):
                        rds += oT(k)[1]
                    sc.op("pe", [(lambda e, k=k, pA=pA, wpa=wpa: e.matmul(pA[:, :], lhsT=wpa[:, k, :], rhs=oT(k)[0], start=(k == 0), stop=(k == 3)))
                                 for k in range(4)], rds, [pAr])
                    pB, pBr = pget(grp)
                    rds = [wr]
                    for k in range(4):
                        rds += sT(k)[1]
                    sc.op("pe", [(lambda e, k=k, pB=pB, wps=wps: e.matmul(pB[:, :], lhsT=wps[:, k, :], rhs=sT(k)[0], start=(k == 0), stop=(k == 3)))
                                 for k in range(4)], rds, [pBr])
                    sc.op("act", lambda e, pC=pC: e.activation(out=tha[:, :], in_=pC[:, :], func=AF.Tanh, scale=0.5), [pCr], [R["tha"]])
                    sc.op("act", lambda e, pD=pD: e.activation(out=thb[:, :], in_=pD[:, :], func=AF.Tanh, scale=0.5), [pDr], [R["thb"]])
                    sc.op("dve", lambda e, pA=pA: e.scalar_tensor_tensor(out=tha[:, :], in0=tha[:, :], scalar=1.0, in1=pA[:, :],
                                                                        op0=ALU.add, op1=ALU.mult), [R["tha"], pAr], [R["tha"]])
                    sc.op("dve", lambda e, pB=pB: e.scalar_tensor_tensor(out=thb[:, :], in0=thb[:, :], scalar=1.0, in1=pB[:, :],
                                                                        op0=ALU.add, op1=ALU.mult), [R["thb"], pBr], [R["thb"]])
                    ma, mr = mT(ct)
                    sc.op("dve", lambda e, ma=ma: e.tensor_tensor(out=ma, in0=tha[:, :], in1=thb[:, :], op=ALU.add),
                          [R["tha"], R["thb"]], mr)

                wo = []
                for dh in range(2):
                    w, wr = load_w(13 + dh)
                    wo.append((w.rearrange("p (k n) -> p k n", k=8), wr))
                stats_begin(8)
                for j in range(NJ):
                    for dh in range(2):
                        w, wr = wo[dh]
                        pb, pr = pget(1)
                        sc.op("pe", [(lambda e, k=k, pb=pb, w=w, j=j: e.matmul(pb[:, :], lhsT=mT(k)[0][:, j * 128:(j + 1) * 128], rhs=w[:, k, :],
                                                                             start=(k == 0), stop=(k == 7))) for k in range(8)],
                              [wr] + [r for k in range(8) for r in mT(k)[1]], [pr])
                        ti = (dh * 4 + j) % 2
                        sc.op("dve", lambda e, pb=pb, ti=ti, dh=dh: e.scalar_tensor_tensor(
                            out=tmp[ti][:, :], in0=pb[:, :], scalar=0.5, in1=g1h_bc[:, dh * 512:(dh + 1) * 512],
                            op0=ALU.mult, op1=ALU.mult), [pr, R["g1h_bc"]], [tmp_res[ti]])
                        sc.op("dve", lambda e, ti=ti, j=j, dh=dh: e.tensor_tensor(out=xs[:, j, dh * 512:(dh + 1) * 512],
                                                                                  in0=xs[:, j, dh * 512:(dh + 1) * 512], in1=tmp[ti][:, :],
                                                                                  op=ALU.add), [tmp_res[ti], Rxs], [Rxs])
                    stats_row(8, j, xs[:, j, :], [Rxs], D)
                    rstd_cols(8 + j, 1, D)
                    norm_xn(8, xs, Rxs, rows=[j])

                ck(t * 10 + 7)
                norm_to_T(8, h2T, R["h2T"], A2, 24, b, xs, Rxs, do_xn=False)
                if idx + 1 < len(tiles):
                    prefetch_A(idx + 1)
                for c8 in range(8):
                    w, wr = load_w(15 + c8)
                    w = w.rearrange("p (k n) -> p k n", k=8)
                    for ct in range(4):
                        pb, pr = mm_feat(w, wr, ct, h2T, R["h2T"])
                        qi = ct % 2
                        aa, aar = aT(c8 * 4 + ct)
                        sc.op("act", lambda e, pb=pb, qi=qi: e.activation(out=sq[qi][:, :], in_=pb[:, :], func=AF.Square), [pr], [sq_res[qi]])
                        sc.op("dve", lambda e, pb=pb, qi=qi, aa=aa: e.scalar_tensor_tensor(out=aa, in0=pb[:, :], scalar=0.0, in1=sq[qi][:, :],
                                                                                          op0=ALU.is_gt, op1=ALU.mult),
                              [pr, sq_res[qi]], aar)
                for dh in range(2):
                    if dh == 1 and idx + 1 < len(tiles):
                        prefetch_B(idx + 1)
                    grp = 1 - dh
                    banks = [pget(grp) for _ in range(NJ)]
                    for kc in range(4):
                        w, wr = load_w(23 + dh * 4 + kc)
                        w = w.rearrange("p (k n) -> p k n", k=8)
                        for j in range(NJ):
                            pb, pr = banks[j]
                            rds = [wr]
                            for k in range(8):
                                rds += aT(kc * 8 + k)[1]
                            sc.op("pe", [(lambda e, k=k, pb=pb, w=w, j=j, kc=kc: e.matmul(
                                pb[:, :], lhsT=aT(kc * 8 + k)[0][:, j * 128:(j + 1) * 128], rhs=w[:, k, :],
                                start=(kc == 0 and k == 0), stop=(kc == 3 and k == 7))) for k in range(8)], rds, [pr])
                    for j in range(NJ):
                        pb, pr = banks[j]
                        ti = (dh * 4 + j) % 2
                        sc.op("dve", lambda e, pb=pb, ti=ti, dh=dh: e.tensor_tensor(out=tmp[ti][:, :], in0=pb[:, :],
                                                                                   in1=g2_bc[:, dh * 512:(dh + 1) * 512], op=ALU.mult),
                              [pr, R["g2_bc"]], [tmp_res[ti]])
                        sc.op("dve", lambda e, ti=ti, j=j, dh=dh: e.tensor_tensor(out=xs[:, j, dh * 512:(dh + 1) * 512],
                                                                                  in0=xs[:, j, dh * 512:(dh + 1) * 512], in1=tmp[ti][:, :],
                                                                                  op=ALU.add), [tmp_res[ti], Rxs], [Rxs])

                ck(t * 10 + 8)
                rms_stats(12, lambda j: xs[:, j, :], lambda j: [Rxs], D)
                for j in range(NJ):
                    sc.op("dve", lambda e, j=j: e.scalar_tensor_tensor(out=xs[:, j, :], in0=xs[:, j, :], scalar=ssq[:, 12 + j:13 + j],
                                                                      in1=gfin_bc[:], op0=ALU.mult, op1=ALU.mult),
                          [Rxs, R["ssq"], R["gfin_bc"]], [Rxs])
                dma("sp", out_d[b, tok0:tok0 + T, :].rearrange("(j p) d -> p j d", p=128), xs[:], [Rxs], [R["out"]], sem=f"out_st{idx % 2}", eng_override="pool")

    except _Stop:
        pass
    for semk in list(sc.cnt.keys()):
        sc.wait_final("sp", semk)

    sems = {k: es.enter_context(nc.semaphore(k)) for k in sc.cnt}

    def replay(items, eng):
        for it in items:
            if it[0] == "w":
                eng.wait_ge(sems[it[1]], it[2])
            else:
                name, a, k = it[1]
                ins = getattr(eng, name)(*a, **k)
                if it[2] is not None:
                    ins.then_inc(sems[it[2]], it[3])

    with nc.Block() as block:
        @block.tensor
        def _(e):
            replay(sc.prog["pe"], e)

        @block.scalar
        def _(e):
            replay(sc.prog["act"], e)

        @block.vector
        def _(e):
            replay(sc.prog["dve"], e)

        @block.gpsimd
        def _(e):
            replay(sc.prog["pool"], e)

        @block.sync
        def _(e):
            replay(sc.prog["sp"], e)
    es.close()
    return nc


def make_consts(NB):
    ident = np.eye(128, dtype=np.float32)
    ki = np.arange(128)[:, None]
    qi = np.arange(128)[None, :]
    tri = np.where(ki <= qi, 0.0, NEG).astype(np.float32)
    maskT = (qi >= ki).astype(np.float32)
    eoh = np.zeros((8, 8, 128), np.float32)
    for j in range(8):
        eoh[j, j, :] = 1.0
    return {"c_ident": ident, "c_tri": tri, "c_maskT": maskT, "c_eoh": eoh.reshape(8, 1024)}


_NC_CACHE = {}


def kernel(x, c, w_ada, b_ada, g_mix, w_in, w_proj_attn, g_sgu, w_sgu, b_sgu, w_proj_sgu, w_out, g_ffn, w_ff1, w_ff2,
           g_final):
    f = lambda a: np.ascontiguousarray(np.asarray(a, dtype=np.float32))
    x = f(x)
    c = f(c)
    B = x.shape[0]
    NB = B // NCORES
    shared = {
        "w_ada": f(w_ada)[0], "b_ada": f(b_ada).reshape(1, -1), "g_mix": f(g_mix).reshape(1, -1), "w_in": f(w_in)[0],
        "w_proj_attn": f(w_proj_attn)[0], "g_sgu": f(g_sgu).reshape(1, -1), "w_sgu": f(w_sgu)[0], "b_sgu": f(b_sgu)[0],
        "w_proj_sgu": f(w_proj_sgu)[0], "w_out": f(w_out)[0], "g_ffn": f(g_ffn).reshape(1, -1), "w_ff1": f(w_ff1)[0],
        "w_ff2": f(w_ff2)[0], "g_final": f(g_final).reshape(1, -1),
    }
    shared.update(make_consts(NB))
    if NB not in _NC_CACHE:
        _NC_CACHE[NB] = build_nc(NB)
    nc = _NC_CACHE[NB]
    in_maps = []
    for i in range(NCORES):
        m = dict(shared)
        m["x"] = x[i * NB:(i + 1) * NB]
        m["c"] = c[i * NB:(i + 1) * NB]
        in_maps.append(m)
    res = run_bass_kernel_spmd(nc, in_maps, core_ids=list(range(NCORES)))
    return np.concatenate([r["out"] for r in res.results], axis=0)
```

```python
from contextlib import ExitStack
import numpy as np
import concourse.bass as bass
import concourse.mybir as mybir
from concourse.bass_utils import run_bass_kernel_spmd

F32 = mybir.dt.float32
BF16 = mybir.dt.bfloat16
AF = mybir.ActivationFunctionType
ALU = mybir.AluOpType
AX = mybir.AxisListType

D = 1024
S = 2048
H = 8
DH = 64
T = 512
NJ = 4
NT = S // T
INW = 4608
DFF = 4096
EPS = 1e-6
NEG = -30000.0
NSLOT = 3
NCORES = 8
CONV_BARRIER = False
USE_SEL = True
SEL_APPLY = True


class Res:
    __slots__ = ("name", "w", "r")

    def __init__(self, name):
        self.name = name
        self.w = None
        self.r = {}


class _Rec:
    def __init__(self):
        self.calls = []

    def __getattr__(self, name):
        def f(*a, **k):
            self.calls.append((name, a, k))
            return self
        return f


def _record(f):
    r = _Rec()
    f(r)
    assert len(r.calls) == 1, r.calls
    return r.calls[0]


class Sched:
    ENG = ("pe", "act", "dve", "pool", "sp")

    def __init__(self):
        self.prog = {e: [] for e in self.ENG}
        self.cnt = {}
        self.waited = {e: {} for e in self.ENG}

    def op(self, eng, fns, reads=(), writes=(), sem=None, inc=1):
        if not isinstance(fns, (list, tuple)):
            fns = [fns]
        own = "s_" + eng
        semk = sem or own
        need = {}

        def add(ev, same_ok):
            if ev is None:
                return
            k, v = ev
            if k == own and not same_ok:
                return
            if need.get(k, 0) < v:
                need[k] = v

        raw_same = eng in ("act", "dve", "pool")
        for r in reads:
            add(r.w, raw_same)
        for w in writes:
            add(w.w, raw_same)
            for k, v in w.r.items():
                add((k, v), raw_same)
        wd = self.waited[eng]
        for k, v in need.items():
            if wd.get(k, 0) < v:
                wd[k] = v
                self.prog[eng].append(("w", k, v))
        self.cnt[semk] = self.cnt.get(semk, 0) + inc
        ev = (semk, self.cnt[semk])
        for f in fns[:-1]:
            self.prog[eng].append(("i", _record(f), None, 0))
        self.prog[eng].append(("i", _record(fns[-1]), semk, inc))
        for r in reads:
            if r.r.get(semk, 0) < ev[1]:
                r.r[semk] = ev[1]
        for w in writes:
            w.w = ev
            w.r = {}
        return ev

    def wait_final(self, eng, semk):
        v = self.cnt.get(semk, 0)
        if v and self.waited[eng].get(semk, 0) < v:
            self.waited[eng][semk] = v
            self.prog[eng].append(("w", semk, v))


class _Stop(Exception):
    pass


def build_nc(NB, stop=None):
    nc = bass.Bass("TRN2", target_bir_lowering=False)

    def ck(n):
        if stop == n:
            raise _Stop()
    sc = Sched()
    es = ExitStack()

    def din(name, shape, dt=F32):
        return nc.dram_tensor(name, list(shape), dt, kind="ExternalInput").ap()

    x_d = din("x", [NB, S, D])
    c_d = din("c", [NB, D])
    wada_d = din("w_ada", [D, 6 * D])
    bada_d = din("b_ada", [1, 6 * D])
    gmix_d = din("g_mix", [1, D])
    win_d = din("w_in", [D, INW])
    wpa_d = din("w_proj_attn", [512, D])
    gsgu_d = din("g_sgu", [1, 512])
    wsgu_d = din("w_sgu", [8, 128, 128])
    bsgu_d = din("b_sgu", [8, 128])
    wps_d = din("w_proj_sgu", [512, D])
    wout_d = din("w_out", [D, D])
    gffn_d = din("g_ffn", [1, D])
    wff1_d = din("w_ff1", [D, DFF])
    wff2_d = din("w_ff2", [DFF, D])
    gfin_d = din("g_final", [1, D])
    identf_d = din("c_ident", [128, 128])
    tri_d = din("c_tri", [128, 128])
    maskT_d = din("c_maskT", [128, 128])
    eoh_d = din("c_eoh", [8, 8 * 128])
    out_d = nc.dram_tensor("out", [NB, S, D], F32, kind="ExternalOutput").ap()

    NCH = 31
    wsc = nc.dram_tensor("wsc", [NCH, 128, 4096], BF16, kind="Internal").ap()
    wsc_res = [Res(f"wsc{i}") for i in range(NCH)]

    def sb(name, shape, dt):
        return es.enter_context(nc.sbuf_tensor(name, list(shape), dt))

    xsb = [sb(f"xs{i}", [128, NJ, D], F32) for i in range(2)]
    xs_res = [Res(f"xs{i}") for i in range(2)]
    junk = sb("junk", [128, D], BF16)
    xn = sb("xn", [128, NJ, D], BF16)
    hT = sb("hT", [128, 8, T], BF16)
    h2T = sb("h2T", [128, 8, T], BF16)
    Qp = sb("Qp", [128, 8, T], BF16)
    Kc = sb("Kc", [128, 4, S], BF16)
    Vc = sb("Vc", [128, 16, 8, 64], BF16)
    ones64 = sb("ones64", [128, 64], BF16)
    arena = sb("arena", [128, 8192], F32)
    arena_b = arena[:].bitcast(BF16)
    ares = [Res(f"ar{i}") for i in range(32)]
    rec = [sb(f"rec{i}", [64, 512], F32) for i in range(2)]
    tha = sb("tha", [128, T], F32)
    thb = sb("thb", [128, T], F32)
    tmp = [sb(f"tmp{i}", [128, 512], F32) for i in range(2)]
    sq = [sb(f"sq{i}", [128, T], F32) for i in range(2)]
    wsl = sb("wsl", [128, NSLOT * 2048], F32)
    identf = sb("identf", [128, 128], F32)
    identb = sb("identb", [128, 128], BF16)
    tri = sb("tri", [128, 128], BF16)
    maskT = sb("maskT", [128, 128], F32)
    eoh = sb("eoh", [128, 8, 128], BF16)
    wcT = sb("wcT", [128, 8, 128], BF16)
    bT = sb("bT", [128, 4, 128], F32)
    gsgu_bc = sb("gsgu_bc", [128, 512], F32)
    gfin_bc = sb("gfin_bc", [128, D], F32)
    g1h_bc = sb("g1h_bc", [128, D], F32)
    g2_bc = sb("g2_bc", [128, D], F32)
    gsc = nc.dram_tensor("gsc", [4, 2, D], F32, kind="Internal").ap()
    modc = [sb(f"modc{i}", [4, 256], F32) for i in range(2)]
    modT = sb("modT", [128, 48, 4], F32)
    gmT = sb("gmT", [128, 8], F32)
    gfT = sb("gfT", [128, 8], F32)
    A1 = sb("A1", [128, 8, 4], F32)
    A2 = sb("A2", [128, 8, 4], F32)
    cactT = sb("cactT", [128, 8, 4], F32)
    ssq = sb("ssq", [128, 16], F32)
    ksum = sb("ksum", [128, 4, 8], F32)
    ksb = sb("ksb", [128, 4, 8], BF16)
    Gs = sb("Gs", [128, 8, 8], F32)
    cmpt = sb("cmpt", [128, 8, 49], F32)
    rank = sb("rank", [128, 8, 8], F32)
    selb4 = [sb(f"selb{j}", [128, 8, 8], BF16) for j in range(NJ)]
    selb_res = [Res(f"selb{j}") for j in range(NJ)]
    selT = sb("selT", [128, 8, T], BF16)

    pbank = [es.enter_context(nc.psum_tensor(f"pb{i}", [128, 512], F32)) for i in range(8)]
    pres = [Res(f"pb{i}") for i in range(8)]
    prr = [0, 0]

    def pget(grp):
        i = grp * 4 + prr[grp] % 4
        prr[grp] += 1
        return pbank[i], pres[i]

    R = {n: Res(n) for n in (
        "junk", "xn", "hT", "h2T", "Qp", "Vc", "tha", "thb", "identf", "identb", "tri", "maskT", "eoh", "gsc",
        "wcT", "bT", "gsgu_bc", "gfin_bc", "g1h_bc", "g2_bc", "modT", "gmT", "gfT", "A1", "A2",
        "cactT", "ones64", "ssq", "ksum", "ksb", "Gs", "cmpt", "rank", "selT", "out")}
    Kc_res = [Res(f"Kc{i}") for i in range(NT)]
    mT_res = [Res(f"mT{i}") for i in range(8)]
    tmp_res = [Res(f"tmp{i}") for i in range(2)]
    sq_res = [Res(f"sq{i}") for i in range(2)]
    rec_res = [Res(f"rec{i}") for i in range(2)]
    modc_res = [Res(f"modc{i}") for i in range(2)]
    wres = [Res(f"wsl{i}") for i in range(NSLOT)]
    wrr = [0]

    def aT(ct):
        return arena_b[:, ct * 512:(ct + 1) * 512], [ares[ct]]

    arena_f = arena[:]
    bada_sb = arena_f[0:4, 1024:7168]
    bada_res = ares[4:28]
    cin = arena_f[0:4, 7168:8192]
    cact = cin
    cin_res = ares[28:32]

    def uT(ct):
        return arena_f[:, ct * 512:(ct + 1) * 512], [ares[2 * ct], ares[2 * ct + 1]]

    def vsg(j):
        return arena_f[:, 2048 + j * 512:2048 + (j + 1) * 512], [ares[8 + 2 * j], ares[9 + 2 * j]]

    def vn(j):
        return arena_b[:, 8192 + j * 512:8192 + (j + 1) * 512], [ares[16 + j]]

    def mT(ct):
        return arena_b[:, 4096 + ct * 512:4096 + (ct + 1) * 512], [ares[8 + ct]]

    def sT(gp):
        return arena_b[:, 10240 + gp * 512:10240 + (gp + 1) * 512], [ares[20 + gp]]

    def oT(pair):
        return arena_b[:, 12288 + pair * 512:12288 + (pair + 1) * 512], [ares[24 + pair]]

    def PT(i):
        return arena_b[:, 14336 + i * 512:14336 + (i + 1) * 512], [ares[28 + i]]

    def wslot_b(s):
        return wsl[:, s * 2048:(s + 1) * 2048].bitcast(BF16)

    dma_n = {"sp": 0, "pool": 0}
    NLANE = {"sp": 8, "pool": 24}
    lane_res = {e: [Res(f"{e}_lane{i}") for i in range(NLANE[e])] for e in NLANE}

    def dma(eng, out, in_, reads, writes, sem=None, ncdma=False, eng_override=None):
        if eng_override is not None:
            eng = eng_override
        if sem is None:
            lane = dma_n[eng] % NLANE[eng]
            dma_n[eng] += 1
            sem = f"{eng}_l{lane}"
            writes = list(writes) + [lane_res[eng][lane]]
        if ncdma:
            f = lambda e, o=out, i=in_: e.dma_start(out=o, in_=i, allow_slow_non_contiguous=True)
        else:
            f = lambda e, o=out, i=in_: e.dma_start(out=o, in_=i)
        return sc.op(eng, f, reads, writes, sem=sem, inc=16)

    try:
        dma("sp", identf[:], identf_d, [], [R["identf"]])
        dma("pool", identb[:], identf_d, [], [R["identb"]])
        dma("pool", tri[:], tri_d, [], [R["tri"]])
        dma("sp", maskT[:], maskT_d, [], [R["maskT"]])
        sc.op("dve", lambda e: e.memset(ones64[:], 1.0), [], [R["ones64"]])
        sc.op("dve", lambda e: e.memset(eoh[:], 0.0), [], [R["eoh"]])
        sc.op("dve", lambda e: e.memset(selT[:], 0.0), [], [R["selT"]])
        sc.op("dve", lambda e: e.memset(Qp[:], 0.0), [], [R["Qp"]])
        dma("pool", eoh[0:8], eoh_d.rearrange("j (k n) -> j k n", k=8), [], [R["eoh"]])
        dma("pool", eoh[64:72], eoh_d.rearrange("j (k n) -> j k n", k=8), [], [R["eoh"]])
        dma("sp", cin[0:NB, :], c_d, [], cin_res)
        dma("sp", bada_sb, bada_d[0:1, :].to_broadcast([4, 6 * D]), [], bada_res)
        dma("sp", gsgu_bc[:], gsgu_d[0:1, :].to_broadcast([128, 512]), [], [R["gsgu_bc"]])
        dma("sp", gfin_bc[:], gfin_d[0:1, :].to_broadcast([128, D]), [], [R["gfin_bc"]])
        dma("sp", gmT[:], gmix_d[0, :].rearrange("(k p) -> p k", p=128), [], [R["gmT"]], ncdma=True)
        dma("sp", gfT[:], gffn_d[0, :].rearrange("(k p) -> p k", p=128), [], [R["gfT"]], ncdma=True)
        for g in range(8):
            dma("sp", bT[(g % 2) * 64:(g % 2) * 64 + 64, g // 2, :], bsgu_d[g:g + 1, :].to_broadcast([64, 128]), [], [R["bT"]])

        def conv(ci, pieces):
            evs = []
            for (c0, ncols, src, nkt) in pieces:
                o = wsc[ci, :, c0:c0 + nkt * ncols].rearrange("p (k n) -> p k n", k=nkt)
                i = src.rearrange("(k p) n -> p k n", p=128)
                evs.append((o, i))
            for n, (o, i) in enumerate(evs):
                dma("pool", o, i, [], [wsc_res[ci]] if n == len(evs) - 1 else [])

        def conv_multi(ci, pieces):
            subs = []
            for n, (c0, ncols, src, nkt) in enumerate(pieces):
                o = wsc[ci, :, c0:c0 + nkt * ncols].rearrange("p (k n) -> p k n", k=nkt)
                i = src.rearrange("(k p) n -> p k n", p=128)
                r = Res(f"wsc{ci}_{n}")
                dma("pool", o, i, [R["modT"]] if ci >= 5 else [], [r])
                subs.append(r)
            return subs

        wsc_sub = {}
        for q in range(5):
            wsc_sub[q] = conv_multi(q, [(0, 512, win_d[:, q * 512:(q + 1) * 512], 8)])
        ck(1)

        sc.op("act", lambda e: e.activation(out=cact[0:NB, :], in_=cin[0:NB, :], func=AF.Silu), cin_res, cin_res)
        pb, pr = pget(0)
        sc.op("pe", [(lambda e, k=k: e.transpose(out=pb[:, k * 4:k * 4 + NB], in_=cact[0:NB, k * 128:(k + 1) * 128],
                                                 identity=identf[0:NB, 0:NB])) for k in range(8)],
              cin_res + [R["identf"]], [pr])
        sc.op("dve", lambda e: e.memset(cactT[:], 0.0), [], [R["cactT"]])
        sc.op("dve", lambda e: e.tensor_copy(out=cactT[:, :, 0:NB], in_=pb[:, 0:32].rearrange("p (k b) -> p k b", b=4)[:, :, 0:NB]),
              [pr], [R["cactT"]])

        NCC = 24
        for cc in range(NCC):
            s = wrr[0] % NSLOT
            wrr[0] += 1
            wv = wsl[:, s * 2048:(s + 1) * 2048].rearrange("p (k n) -> p k n", k=8)
            dma("sp", wv, wada_d[:, cc * 256:(cc + 1) * 256].rearrange("(k p) n -> p k n", p=128), [], [wres[s]], sem=f"w{s}")
            pb, pr = pget(0)
            sc.op("pe", [(lambda e, k=k, wv=wv, pb=pb: e.matmul(pb[0:4, 0:256], lhsT=cactT[:, k, :], rhs=wv[:, k, :],
                                                              start=(k == 0), stop=(k == 7))) for k in range(8)],
                  [wres[s], R["cactT"]], [pr])
            mi = cc % 2
            sc.op("dve", lambda e, pb=pb, mi=mi, cc=cc: e.tensor_tensor(out=modc[mi][:, 0:256], in0=pb[0:4, 0:256],
                                                                        in1=bada_sb[:, cc * 256:(cc + 1) * 256], op=ALU.add),
                  [pr] + bada_res, [modc_res[mi]])
            col = cc * 256
            if 2048 <= col < 3072 or 5120 <= col < 6144:
                gi = 0 if col < 3072 else 1
                off = col - (2048 if gi == 0 else 5120)
                dma("sp", gsc[:, gi, off:off + 256], modc[mi][:, 0:256], [modc_res[mi]], [R["gsc"]])
            pb2, pr2 = pget(1)
            sc.op("pe", [(lambda e, q=q, mi=mi, pb2=pb2: e.transpose(out=pb2[:, q * 4:q * 4 + 4], in_=modc[mi][:, q * 128:(q + 1) * 128],
                                                                    identity=identf[0:4, 0:4])) for q in range(2)],
                  [modc_res[mi], R["identf"]], [pr2])
            sc.op("act", lambda e, cc=cc, pb2=pb2: e.activation(out=modT[:, cc * 2:cc * 2 + 2, :],
                                                               in_=pb2[:, 0:8].rearrange("p (q b) -> p q b", b=4), func=AF.Copy),
                  [pr2], [R["modT"]])
        ck(2)

        for ct in range(8):
            wsc_sub[5 + ct] = conv_multi(5 + ct, [
                (0, 128, win_d[:, 2560 + ct * 128:2560 + (ct + 1) * 128], 8),
                (1024, 128, win_d[:, 3584 + ct * 128:3584 + (ct + 1) * 128], 8),
                (2048, 128, wpa_d[:, ct * 128:(ct + 1) * 128], 4),
                (2560, 128, wps_d[:, ct * 128:(ct + 1) * 128], 4)])
        for dh in range(2):
            wsc_sub[13 + dh] = conv_multi(13 + dh, [(0, 512, wout_d[:, dh * 512:(dh + 1) * 512], 8)])
        for c8 in range(8):
            wsc_sub[15 + c8] = conv_multi(15 + c8, [(0, 512, wff1_d[:, c8 * 512:(c8 + 1) * 512], 8)])
        for dh in range(2):
            for kc in range(4):
                wsc_sub[23 + dh * 4 + kc] = conv_multi(23 + dh * 4 + kc,
                                                       [(0, 512, wff2_d[kc * 1024:(kc + 1) * 1024, dh * 512:(dh + 1) * 512], 8)])

        sc.op("dve", lambda e: e.scalar_tensor_tensor(out=A1[:], in0=modT[:, 8:16, :], scalar=1.0,
                                                      in1=gmT[:].unsqueeze(2).to_broadcast([128, 8, 4]), op0=ALU.add, op1=ALU.mult),
              [R["modT"], R["gmT"]], [R["A1"]])
        sc.op("dve", lambda e: e.scalar_tensor_tensor(out=A2[:], in0=modT[:, 32:40, :], scalar=1.0,
                                                      in1=gfT[:].unsqueeze(2).to_broadcast([128, 8, 4]), op0=ALU.add, op1=ALU.mult),
              [R["modT"], R["gfT"]], [R["A2"]])

        wtmp = arena_f[:, 0:1024].rearrange("p (g s) -> p g s", g=8)
        wtmp_res = [ares[0], ares[1], ares[2], ares[3]]
        dma("sp", wtmp, wsgu_d.rearrange("g t s -> t g s"), [], wtmp_res)
        for half in range(2):
            pb, pr = pget(0)
            sc.op("pe", [(lambda e, g=g, pb=pb, half=half: e.transpose(out=pb[:, g * 128:(g + 1) * 128], in_=wtmp[:, half * 4 + g, :],
                                                                      identity=identf[:])) for g in range(4)],
                  wtmp_res + [R["identf"]], [pr])
            sc.op("dve", lambda e, pb=pb, half=half: e.tensor_tensor(
                out=wcT[:, half * 4:half * 4 + 4, :], in0=pb[:, 0:512].rearrange("p (g t) -> p g t", g=4),
                in1=maskT[:].unsqueeze(1).to_broadcast([128, 4, 128]), op=ALU.mult),
                [pr, R["maskT"]], [R["wcT"]])

        sc.op("dve", lambda e: e.memset(ksum[:], 0.0), [], [R["ksum"]])
        sc.op("dve", lambda e: e.memset(ksb[:], 0.0), [], [R["ksb"]])
        ck(3)

        def load_w(ci, n=4096):
            s = wrr[0] % NSLOT
            wrr[0] += 1
            dma("sp", wslot_b(s)[:, 0:n], wsc[ci, :, 0:n], wsc_sub[ci], [wres[s]], sem=f"w{s}")
            return wslot_b(s), wres[s]

        def stats_begin(col0, n=NJ):
            sc.op("dve", lambda e: e.memset(ssq[:, col0:col0 + n], 0.0), [], [R["ssq"]])

        def stats_row(col0, j, src, src_res, width):
            sc.op("act", lambda e: e.activation(out=junk[:, 0:width], in_=src, func=AF.Square,
                                                accum_out=ssq[:, col0 + j:col0 + j + 1]),
                  list(src_res) + [R["ssq"]], [R["junk"], R["ssq"]])

        def rms_stats(col0, src_fn, src_res, width, n=NJ, rows=True):
            if rows:
                stats_begin(col0, n)
                for j in range(n):
                    stats_row(col0, j, src_fn(j), src_res(j), width)
            rstd_cols(col0, n, width)

        def rstd_cols(col0, n, width):
            sc.op("dve", lambda e: e.tensor_scalar(out=ssq[:, col0:col0 + n], in0=ssq[:, col0:col0 + n], scalar1=1.0 / width,
                                                   scalar2=EPS, op0=ALU.mult, op1=ALU.add), [R["ssq"]], [R["ssq"]])
            sc.op("act", lambda e: e.activation(out=ssq[:, col0:col0 + n], in_=ssq[:, col0:col0 + n], func=AF.Sqrt),
                  [R["ssq"]], [R["ssq"]])
            sc.op("dve", lambda e: e.reciprocal(out=ssq[:, col0:col0 + n], in_=ssq[:, col0:col0 + n]), [R["ssq"]], [R["ssq"]])

        def norm_xn(col0, xs, Rxs, rows=None):
            for j in (range(NJ) if rows is None else rows):
                sc.op("dve", lambda e, j=j: e.tensor_scalar(out=xn[:, j, :], in0=xs[:, j, :], scalar1=ssq[:, col0 + j:col0 + j + 1],
                                                            scalar2=None, op0=ALU.mult), [Rxs, R["ssq"]], [R["xn"]])

        def norm_to_T(col0, dstT, dst_res, Amod, shcol, b, xs, Rxs, do_xn=True):
            if do_xn:
                norm_xn(col0, xs, Rxs)
            for k in range(8):
                pb, pr = pget(0)
                pbb = pb[:].bitcast(BF16)
                sc.op("pe", [(lambda e, j=j, k=k, pbb=pbb: e.transpose(out=pbb[:, j * 128:(j + 1) * 128],
                                                                      in_=xn[:, j, k * 128:(k + 1) * 128], identity=identb[:]))
                             for j in range(NJ)], [R["xn"], R["identb"]], [pr])
                sc.op("act", lambda e, k=k, pbb=pbb: e.activation(out=dstT[:, k, :], in_=pbb[:, 0:T], func=AF.Identity,
                                                                 scale=Amod[:, k, b:b + 1], bias=modT[:, shcol + k, b:b + 1]),
                      [pr, R["A1"], R["A2"], R["modT"]], [dst_res])

        def mm_feat(w, wr, ct, src, src_res):
            pb, pr = pget(0)
            sc.op("pe", [(lambda e, k=k, pb=pb: e.matmul(pb[:, :], lhsT=w[:, k, ct * 128:(ct + 1) * 128], rhs=src[:, k, :],
                                                       start=(k == 0), stop=(k == 7))) for k in range(8)],
                  [wr, src_res], [pr])
            return pb, pr

        def mm_tok(w, wr, j, src, src_res, grp=0):
            pb, pr = pget(grp)
            sc.op("pe", [(lambda e, k=k, pb=pb: e.matmul(pb[:, :], lhsT=src[:, k, j * 128:(j + 1) * 128], rhs=w[:, k, :],
                                                       start=(k == 0), stop=(k == 7))) for k in range(8)],
                  [wr, src_res], [pr])
            return pb, pr

        tiles = [(bb, tt) for bb in range(NB) for tt in range(NT)]

        def prefetch_A(idx):
            bb, tt = tiles[idx]
            p = idx % 2
            xs_, Rx = xsb[p], xs_res[p]
            dma("sp", xs_[:], x_d[bb, tt * T:(tt + 1) * T, :].rearrange("(j p) d -> p j d", p=128), [], [Rx], sem=f"x_ld{p}")
            rms_stats(0, lambda j: xs_[:, j, :], lambda j: [Rx], D)
            norm_xn(0, xs_, Rx)

        def prefetch_B(idx):
            bb, tt = tiles[idx]
            p = idx % 2
            norm_to_T(0, hT, R["hT"], A1, 0, bb, xsb[p], xs_res[p], do_xn=False)

        prefetch_A(0)
        prefetch_B(0)
        for idx, (b, t) in enumerate(tiles):
            if True:
                tok0 = t * T
                xs, Rxs = xsb[idx % 2], xs_res[idx % 2]
                if t == 0:
                    gq = "sp" if b == 0 else "pool"
                    dma(gq, g1h_bc[:], gsc[b:b + 1, 0, :].to_broadcast([128, D]), [R["gsc"]], [R["g1h_bc"]])
                    dma(gq, g2_bc[:], gsc[b:b + 1, 1, :].to_broadcast([128, D]), [R["gsc"]], [R["g2_bc"]])
                ck(t * 10 + 4)

                w, wr = load_w(4)
                w = w.rearrange("p (k n) -> p k n", k=8)
                for j in range(NJ):
                    pb, pr = mm_tok(w, wr, j, hT, R["hT"])
                    va, vr = vsg(j)
                    sc.op("act", lambda e, pb=pb, va=va: e.activation(out=va, in_=pb[:, :], func=AF.Gelu_apprx_tanh), [pr], vr)
                rms_stats(4, lambda j: vsg(j)[0], lambda j: vsg(j)[1], 512)
                for j in range(NJ):
                    va, vr = vsg(j)
                    na, nr = vn(j)
                    sc.op("dve", lambda e, va=va, na=na, j=j: e.scalar_tensor_tensor(out=na, in0=va, scalar=ssq[:, 4 + j:5 + j],
                                                                                    in1=gsgu_bc[:], op0=ALU.mult, op1=ALU.mult),
                          vr + [R["ssq"], R["gsgu_bc"]], nr)
                w, wr = load_w(3)
                w = w.rearrange("p (k n) -> p k n", k=8)
                for ct in range(4):
                    pb, pr = mm_feat(w, wr, ct, hT, R["hT"])
                    ua, ur = uT(ct)
                    sc.op("act", lambda e, pb=pb, ua=ua: e.activation(out=ua, in_=pb[:, :], func=AF.Gelu_apprx_tanh), [pr], ur)
                w, wr = load_w(0)
                w = w.rearrange("p (k n) -> p k n", k=8)
                for ct in range(4):
                    pb, pr = mm_feat(w, wr, ct, hT, R["hT"])
                    for par in range(2):
                        sc.op("dve", lambda e, pb=pb, ct=ct, par=par: e.tensor_scalar(
                            out=Qp[par * 64:(par + 1) * 64, 2 * ct + par, :], in0=pb[par * 64:(par + 1) * 64, :], scalar1=0.125,
                            scalar2=None, op0=ALU.mult), [pr], [R["Qp"]])
                w, wr = load_w(1)
                w = w.rearrange("p (k n) -> p k n", k=8)
                sc.op("dve", lambda e: e.memset(ksum[:, :, 2 * t:2 * t + 2], 0.0), [], [R["ksum"]])
                for ct in range(4):
                    pb, pr = mm_feat(w, wr, ct, hT, R["hT"])
                    sc.op("act", [(lambda e, pb=pb, ct=ct, bl=bl: e.activation(
                        out=Kc[:, ct, tok0 + bl * 256:tok0 + (bl + 1) * 256], in_=pb[:, bl * 256:(bl + 1) * 256], func=AF.Copy,
                        accum_out=ksum[:, ct, 2 * t + bl:2 * t + bl + 1])) for bl in range(2)],
                          [pr, R["ksum"]], [Kc_res[t], R["ksum"]])
                sc.op("dve", lambda e: e.tensor_copy(out=ksb[:, :, 2 * t:2 * t + 2], in_=ksum[:, :, 2 * t:2 * t + 2]),
                      [R["ksum"]], [R["ksb"]])
                ck(t * 10 + 5)
                use_sel = (t >= 2) and USE_SEL
                if use_sel:
                    gbank = []
                    for j in range(NJ):
                        qb = 2 * t + j // 2
                        pb, pr = pget(1)
                        gbank.append((pb, pr))
                        sc.op("pe", [(lambda e, pb=pb, h=h, j=j, qb=qb: e.matmul(
                            pb[:, h * 8:h * 8 + qb], lhsT=Qp[:, h, j * 128:(j + 1) * 128],
                            rhs=ksb[:, h // 2, 0:qb], start=True, stop=True)) for h in range(H)],
                              [R["Qp"], R["ksb"]], [pr])
                    for j in range(NJ):
                        qb = 2 * t + j // 2
                        pb, pr = gbank[j]
                        sc.op("dve", lambda e, pb=pb, qb=qb: e.tensor_copy(
                            out=Gs[:, :, 0:qb], in_=pb[:, 0:64].rearrange("p (h n) -> p h n", n=8)[:, :, 0:qb]), [pr], [R["Gs"]])
                        sc.op("dve", lambda e, qb=qb: e.tensor_tensor(
                            out=cmpt[:, :, 0:qb * qb].rearrange("p h (a m) -> p h a m", m=qb),
                            in0=Gs[:, :, 0:qb].unsqueeze(2).to_broadcast([128, 8, qb, qb]),
                            in1=Gs[:, :, 0:qb].unsqueeze(3).to_broadcast([128, 8, qb, qb]), op=ALU.is_gt),
                            [R["Gs"]], [R["cmpt"]])
                        sc.op("dve", lambda e, qb=qb: e.reduce_sum(out=rank[:, :, 0:qb],
                                                                  in_=cmpt[:, :, 0:qb * qb].rearrange("p h (a m) -> p h a m", m=qb),
                                                                  axis=AX.X), [R["cmpt"]], [R["rank"]])
                        sc.op("dve", lambda e, j=j: e.memset(selb4[j][:], 0.0), [], [selb_res[j]])
                        sc.op("dve", lambda e, qb=qb, j=j: e.tensor_scalar(out=selb4[j][:, :, 0:qb], in0=rank[:, :, 0:qb], scalar1=2.5,
                                                                          scalar2=NEG, op0=ALU.is_ge, op1=ALU.mult),
                              [R["rank"], selb_res[j]], [selb_res[j]])

                w, wr = load_w(2)
                w = w.rearrange("p (k n) -> p k n", k=8)
                for j in range(NJ):
                    pb, pr = mm_tok(w, wr, j, hT, R["hT"])
                    sc.op("dve", lambda e, pb=pb, j=j: e.tensor_copy(out=Vc[:, 4 * t + j, 0:8, :],
                                                                    in_=pb[:, :].rearrange("p (h d) -> p h d", d=64)),
                          [pr], [R["Vc"]])
                for gp in range(4):
                    pb, pr = pget(1)
                    fns = []
                    rds = [R["wcT"]]
                    for j in range(NJ):
                        na, nr = vn(j)
                        rds += nr
                        for hf in range(2):
                            g = 2 * gp + hf
                            fns.append(lambda e, pb=pb, na=na, j=j, hf=hf, g=g: e.matmul(
                                pb[hf * 64:(hf + 1) * 64, j * 128:(j + 1) * 128], lhsT=na[:, g * 64:(g + 1) * 64], rhs=wcT[:, g, :],
                                start=True, stop=True, tile_position=(0, hf * 64)))
                    sc.op("pe", fns, rds, [pr])
                    sa, sr = sT(gp)
                    ua, ur = uT(gp)
                    t0, t0r = tmp[gp % 2], tmp_res[gp % 2]
                    sc.op("dve", lambda e, pb=pb, t0=t0, gp=gp: e.tensor_tensor(
                        out=t0[:, :].rearrange("p (j t) -> p j t", t=128), in0=pb[:, :].rearrange("p (j t) -> p j t", t=128),
                        in1=bT[:, gp:gp + 1, :].to_broadcast([128, 4, 128]), op=ALU.add), [pr, R["bT"]], [t0r])
                    sc.op("dve", lambda e, t0=t0, sa=sa, ua=ua: e.tensor_tensor(out=sa, in0=t0[:, :], in1=ua, op=ALU.mult),
                          [t0r] + ur, sr)

                if use_sel:
                    for j in range(NJ):
                        pb2, pr2 = pget(1)
                        sc.op("pe", [(lambda e, h=h, pb2=pb2, j=j: e.matmul(
                            pb2[(h % 2) * 64:(h % 2) * 64 + 8, (h // 2) * 128:(h // 2 + 1) * 128], lhsT=selb4[j][:, h, :], rhs=identb[:],
                            start=True, stop=True, tile_position=(0, (h % 2) * 64))) for h in range(H)],
                              [selb_res[j], R["identb"]], [pr2])
                        for par in range(2):
                            sc.op("dve", lambda e, j=j, pb2=pb2, par=par: e.tensor_copy(
                                out=selT[par * 64:par * 64 + 8, par::2, j * 128:(j + 1) * 128],
                                in_=pb2[par * 64:par * 64 + 8, :].rearrange("p (h q) -> p h q", h=4)),
                                  [pr2], [R["selT"]])

                steps = []
                for h in range(H):
                    ents = []
                    for kt in range(4 * t):
                        ents.append((kt, 0, 512, None, (0, 512) if use_sel else None))
                    ents.append((4 * t, 0, 512, (0, 128), (256, 512) if use_sel else None))
                    ents.append((4 * t + 1, 128, 512, (128, 256), (256, 512) if use_sel else None))
                    ents.append((4 * t + 2, 256, 512, (256, 384), None))
                    ents.append((4 * t + 3, 384, 512, (384, 512), None))
                    for i, en in enumerate(ents):
                        steps.append((h, i, len(ents), en))
                acc_of = {}

                def emit_scores(si):
                    h, i, n, (kt, c0, c1, tric, biasc) = steps[si]
                    off, pair = (h % 2) * 64, h // 2
                    pb, pr = pget(0)
                    fns = []
                    rds = [R["Qp"], Kc_res[kt // 4]]
                    extra = (tric is not None) or (biasc is not None and SEL_APPLY)
                    nextra = (1 if tric is not None else 0) + (1 if (biasc is not None and SEL_APPLY) else 0)
                    fns.append(lambda e: e.matmul(pb[:, c0:c1], lhsT=Kc[:, pair, kt * 128:(kt + 1) * 128],
                                                  rhs=Qp[:, h, c0:c1], start=True, stop=not extra))
                    if biasc is not None and SEL_APPLY:
                        nextra -= 1
                        rds += [R["eoh"], R["selT"]]
                        fns.append(lambda e, last=(nextra == 0): e.matmul(
                            pb[:, biasc[0]:biasc[1]], lhsT=eoh[:, kt // 2, :],
                            rhs=selT[:, h, biasc[0]:biasc[1]], start=False, stop=last))
                    if tric is not None:
                        rds += [R["identb"], R["tri"]]
                        fns.append(lambda e: e.matmul(pb[:, tric[0]:tric[1]], lhsT=identb[:], rhs=tri[:], start=False, stop=True))
                    sc.op("pe", fns, rds, [pr])
                    return pb, pr

                def emit_exp_pv(si, pb, pr, pslot):
                    h, i, n, (kt, c0, c1, tric, biasc) = steps[si]
                    off, pair = (h % 2) * 64, h // 2
                    pa, par = PT(pslot)
                    sc.op("act", lambda e: e.activation(out=pa[:, c0:c1], in_=pb[:, c0:c1], func=AF.Exp), [pr], par)
                    if i == 0:
                        acc_of[h] = pget(1)
                    ab, ar = acc_of[h]
                    first, last = (i == 0), (i == n - 1)
                    sc.op("pe", [
                        lambda e: e.matmul(ab[0:64, c0:c1], lhsT=Vc[:, kt, h, :], rhs=pa[:, c0:c1], start=first, stop=last,
                                           tile_position=(0, 0)),
                        lambda e: e.matmul(ab[64:128, c0:c1], lhsT=ones64[:, :], rhs=pa[:, c0:c1], start=first, stop=last,
                                           tile_position=(0, 64))],
                          par + [R["Vc"], R["ones64"]], [ar])
                    if last:
                        ri = h % 2
                        sc.op("dve", lambda e: e.reciprocal(out=rec[ri][:, :], in_=ab[64:128, :]), [ar], [rec_res[ri]])
                        oa, orr = oT(pair)
                        sc.op("dve", lambda e: e.tensor_tensor(out=oa[off:off + 64, :], in0=ab[0:64, :], in1=rec[ri][:, :],
                                                               op=ALU.mult), [ar, rec_res[ri]], orr)

                nst = len(steps)
                LOOK = 1
                pend = [emit_scores(si) for si in range(min(LOOK, nst))]
                for si in range(nst):
                    if si + LOOK < nst:
                        pend.append(emit_scores(si + LOOK))
                    pb_, pr_ = pend.pop(0)
                    emit_exp_pv(si, pb_, pr_, si % 4)

                ck(t * 10 + 6)
                for ct in range(8):
                    w, wr = load_w(5 + ct, 3072)
                    wga = w[:, 0:1024].rearrange("p (k n) -> p k n", k=8)
                    wgb = w[:, 1024:2048].rearrange("p (k n) -> p k n", k=8)
                    wpa = w[:, 2048:2560].rearrange("p (k n) -> p k n", k=4)
                    wps = w[:, 2560:3072].rearrange("p (k n) -> p k n", k=4)
                    grp = ct % 2
                    pC, pCr = pget(grp)
                    sc.op("pe", [(lambda e, k=k, pC=pC, wga=wga: e.matmul(pC[:, :], lhsT=wga[:, k, :], rhs=hT[:, k, :], start=(k == 0), stop=(k == 7)))
                                 for k in range(8)], [wr, R["hT"]], [pCr])
                    pD, pDr = pget(grp)
                    sc.op("pe", [(lambda e, k=k, pD=pD, wgb=wgb: e.matmul(pD[:, :], lhsT=wgb[:, k, :], rhs=hT[:, k, :], start=(k == 0), stop=(k == 7)))
                                 for k in range(8)], [wr, R["hT"]], [pDr])
                    pA, pAr = pget(grp)
                    rds = [wr]
                    for k in range(4):
                        rds += oT(k)[1]
                    sc.op("pe", [(lambda e, k=k, pA=pA, wpa=wpa: e.matmul(pA[:, :], lhsT=wpa[:, k, :], rhs=oT(k)[0], start=(k == 0), stop=(k == 3)))
                                 for k in range(4)], rds, [pAr])
                    pB, pBr = pget(grp)
                    rds = [wr]
                    for k in range(4):
                        rds += sT(k)[1]
                    sc.op("pe", [(lambda e, k=k, pB=pB, wps=wps: e.matmul(pB[:, :], lhsT=wps[:, k, :], rhs=sT(k)[0], start=(k == 0), stop=(k == 3)))
                                 for k in range(4)], rds, [pBr])
                    sc.op("act", lambda e, pC=pC: e.activation(out=tha[:, :], in_=pC[:, :], func=AF.Tanh, scale=0.5), [pCr], [R["tha"]])
                    sc.op("act", lambda e, pD=pD: e.activation(out=thb[:, :], in_=pD[:, :], func=AF.Tanh, scale=0.5), [pDr], [R["thb"]])
                    sc.op("dve", lambda e, pA=pA: e.scalar_tensor_tensor(out=tha[:, :], in0=tha[:, :], scalar=1.0, in1=pA[:, :],
                                                                        op0=ALU.add, op1=ALU.mult), [R["tha"], pAr], [R["tha"]])
                    sc.op("dve", lambda e, pB=pB: e.scalar_tensor_tensor(out=thb[:, :], in0=thb[:, :], scalar=1.0, in1=pB[:, :],
                                                                        op0=ALU.add, op1=ALU.mult), [R["thb"], pBr], [R["thb"]])
                    ma, mr = mT(ct)
                    sc.op("dve", lambda e, ma=ma: e.tensor_tensor(out=ma, in0=tha[:, :], in1=thb[:, :], op=ALU.add),
                          [R["tha"], R["thb"]], mr)

                wo = []
                for dh in range(2):
                    w, wr = load_w(13 + dh)
                    wo.append((w.rearrange("p (k n) -> p k n", k=8), wr))
                stats_begin(8)
                for j in range(NJ):
                    for dh in range(2):
                        w, wr = wo[dh]
                        pb, pr = pget(1)
                        sc.op("pe", [(lambda e, k=k, pb=pb, w=w, j=j: e.matmul(pb[:, :], lhsT=mT(k)[0][:, j * 128:(j + 1) * 128], rhs=w[:, k, :],
                                                                             start=(k == 0), stop=(k == 7))) for k in range(8)],
                              [wr] + [r for k in range(8) for r in mT(k)[1]], [pr])
                        ti = (dh * 4 + j) % 2
                        sc.op("dve", lambda e, pb=pb, ti=ti, dh=dh: e.scalar_tensor_tensor(
                            out=tmp[ti][:, :], in0=pb[:, :], scalar=0.5, in1=g1h_bc[:, dh * 512:(dh + 1) * 512],
                            op0=ALU.mult, op1=ALU.mult), [pr, R["g1h_bc"]], [tmp_res[ti]])
                        sc.op("dve", lambda e, ti=ti, j=j, dh=dh: e.tensor_tensor(out=xs[:, j, dh * 512:(dh + 1) * 512],
                                                                                  in0=xs[:, j, dh * 512:(dh + 1) * 512], in1=tmp[ti][:, :],
                                                                                  op=ALU.add), [tmp_res[ti], Rxs], [Rxs])
                    stats_row(8, j, xs[:, j, :], [Rxs], D)
                    rstd_cols(8 + j, 1, D)
                    norm_xn(8, xs, Rxs, rows=[j])

                ck(t * 10 + 7)
                norm_to_T(8, h2T, R["h2T"], A2, 24, b, xs, Rxs, do_xn=False)
                if idx + 1 < len(tiles):
                    prefetch_A(idx + 1)
                for c8 in range(8):
                    w, wr = load_w(15 + c8)
                    w = w.rearrange("p (k n) -> p k n", k=8)
                    for ct in range(4):
                        pb, pr = mm_feat(w, wr, ct, h2T, R["h2T"])
                        qi = ct % 2
                        aa, aar = aT(c8 * 4 + ct)
                        sc.op("act", lambda e, pb=pb, qi=qi: e.activation(out=sq[qi][:, :], in_=pb[:, :], func=AF.Square), [pr], [sq_res[qi]])
                        sc.op("dve", lambda e, pb=pb, qi=qi, aa=aa: e.scalar_tensor_tensor(out=aa, in0=pb[:, :], scalar=0.0, in1=sq[qi][:, :],
                                                                                          op0=ALU.is_gt, op1=ALU.mult),
                              [pr, sq_res[qi]], aar)
                for dh in range(2):
                    if dh == 1 and idx + 1 < len(tiles):
                        prefetch_B(idx + 1)
                    grp = 1 - dh
                    banks = [pget(grp) for _ in range(NJ)]
                    for kc in range(4):
                        w, wr = load_w(23 + dh * 4 + kc)
                        w = w.rearrange("p (k n) -> p k n", k=8)
                        for j in range(NJ):
                            pb, pr = banks[j]
                            rds = [wr]
                            for k in range(8):
                                rds += aT(kc * 8 + k)[1]
                            sc.op("pe", [(lambda e, k=k, pb=pb, w=w, j=j, kc=kc: e.matmul(
                                pb[:, :], lhsT=aT(kc * 8 + k)[0][:, j * 128:(j + 1) * 128], rhs=w[:, k, :],
                                start=(kc == 0 and k == 0), stop=(kc == 3 and k == 7))) for k in range(8)], rds, [pr])
                    for j in range(NJ):
                        pb, pr = banks[j]
                        ti = (dh * 4 + j) % 2
                        sc.op("dve", lambda e, pb=pb, ti=ti, dh=dh: e.tensor_tensor(out=tmp[ti][:, :], in0=pb[:, :],
                                                                                   in1=g2_bc[:, dh * 512:(dh + 1) * 512], op=ALU.mult),
                              [pr, R["g2_bc"]], [tmp_res[ti]])
                        sc.op("dve", lambda e, ti=ti, j=j, dh=dh: e.tensor_tensor(out=xs[:, j, dh * 512:(dh + 1) * 512],
                                                                                  in0=xs[:, j, dh * 512:(dh + 1) * 512], in1=tmp[ti][:, :],
                                                                                  op=ALU.add), [tmp_res[ti], Rxs], [Rxs])

                ck(t * 10 + 8)
                rms_stats(12, lambda j: xs[:, j, :], lambda j: [Rxs], D)
                for j in range(NJ):
                    sc.op("dve", lambda e, j=j: e.scalar_tensor_tensor(out=xs[:, j, :], in0=xs[:, j, :], scalar=ssq[:, 12 + j:13 + j],
                                                                      in1=gfin_bc[:], op0=ALU.mult, op1=ALU.mult),
                          [Rxs, R["ssq"], R["gfin_bc"]], [Rxs])
                dma("sp", out_d[b, tok0:tok0 + T, :].rearrange("(j p) d -> p j d", p=128), xs[:], [Rxs], [R["out"]], sem=f"out_st{idx % 2}", eng_override="pool")

    except _Stop:
        pass
    for semk in list(sc.cnt.keys()):
        sc.wait_final("sp", semk)

    sems = {k: es.enter_context(nc.semaphore(k)) for k in sc.cnt}

    def replay(items, eng):
        for it in items:
            if it[0] == "w":
                eng.wait_ge(sems[it[1]], it[2])
            else:
                name, a, k = it[1]
                ins = getattr(eng, name)(*a, **k)
                if it[2] is not None:
                    ins.then_inc(sems[it[2]], it[3])

    with nc.Block() as block:
        @block.tensor
        def _(e):
            replay(sc.prog["pe"], e)

        @block.scalar
        def _(e):
            replay(sc.prog["act"], e)

        @block.vector
        def _(e):
            replay(sc.prog["dve"], e)

        @block.gpsimd
        def _(e):
            replay(sc.prog["pool"], e)

        @block.sync
        def _(e):
            replay(sc.prog["sp"], e)
    es.close()
    return nc


def make_consts(NB):
    ident = np.eye(128, dtype=np.float32)
    ki = np.arange(128)[:, None]
    qi = np.arange(128)[None, :]
    tri = np.where(ki <= qi, 0.0, NEG).astype(np.float32)
    maskT = (qi >= ki).astype(np.float32)
    eoh = np.zeros((8, 8, 128), np.float32)
    for j in range(8):
        eoh[j, j, :] = 1.0
    return {"c_ident": ident, "c_tri": tri, "c_maskT": maskT, "c_eoh": eoh.reshape(8, 1024)}


_NC_CACHE = {}


def kernel(x, c, w_ada, b_ada, g_mix, w_in, w_proj_attn, g_sgu, w_sgu, b_sgu, w_proj_sgu, w_out, g_ffn, w_ff1, w_ff2,
           g_final):
    f = lambda a: np.ascontiguousarray(np.asarray(a, dtype=np.float32))
    x = f(x)
    c = f(c)
    B = x.shape[0]
    NB = B // NCORES
    shared = {
        "w_ada": f(w_ada)[0], "b_ada": f(b_ada).reshape(1, -1), "g_mix": f(g_mix).reshape(1, -1), "w_in": f(w_in)[0],
        "w_proj_attn": f(w_proj_attn)[0], "g_sgu": f(g_sgu).reshape(1, -1), "w_sgu": f(w_sgu)[0], "b_sgu": f(b_sgu)[0],
        "w_proj_sgu": f(w_proj_sgu)[0], "w_out": f(w_out)[0], "g_ffn": f(g_ffn).reshape(1, -1), "w_ff1": f(w_ff1)[0],
        "w_ff2": f(w_ff2)[0], "g_final": f(g_final).reshape(1, -1),
    }
    shared.update(make_consts(NB))
    if NB not in _NC_CACHE:
        _NC_CACHE[NB] = build_nc(NB)
    nc = _NC_CACHE[NB]
    in_maps = []
    for i in range(NCORES):
        m = dict(shared)
        m["x"] = x[i * NB:(i + 1) * NB]
        m["c"] = c[i * NB:(i + 1) * NB]
        in_maps.append(m)
    res = run_bass_kernel_spmd(nc, in_maps, core_ids=list(range(NCORES)))
    return np.concatenate([r["out"] for r in res.results], axis=0)
```

```python
from contextlib import ExitStack
import numpy as np
import concourse.bass as bass
import concourse.mybir as mybir
from concourse.bass_utils import run_bass_kernel_spmd

F32 = mybir.dt.float32
BF16 = mybir.dt.bfloat16
AF = mybir.ActivationFunctionType
ALU = mybir.AluOpType
AX = mybir.AxisListType

D = 1024
S = 2048
H = 8
DH = 64
T = 512
NJ = 4
NT = S // T
INW = 4608
DFF = 4096
EPS = 1e-6
NEG = -30000.0
NSLOT = 3
NCORES = 8
CONV_BARRIER = False
USE_SEL = True
SEL_APPLY = True


class Res:
    __slots__ = ("name", "w", "r")

    def __init__(self, name):
        self.name = name
        self.w = None
        self.r = {}


class _Rec:
    def __init__(self):
        self.calls = []

    def __getattr__(self, name):
        def f(*a, **k):
            self.calls.append((name, a, k))
            return self
        return f


def _record(f):
    r = _Rec()
    f(r)
    assert len(r.calls) == 1, r.calls
    return r.calls[0]


class Sched:
    ENG = ("pe", "act", "dve", "pool", "sp")

    def __init__(self):
        self.prog = {e: [] for e in self.ENG}
        self.cnt = {}
        self.waited = {e: {} for e in self.ENG}

    def op(self, eng, fns, reads=(), writes=(), sem=None, inc=1):
        if not isinstance(fns, (list, tuple)):
            fns = [fns]
        own = "s_" + eng
        semk = sem or own
        need = {}

        def add(ev, same_ok):
            if ev is None:
                return
            k, v = ev
            if k == own and not same_ok:
                return
            if need.get(k, 0) < v:
                need[k] = v

        raw_same = eng in ("act", "dve", "pool")
        for r in reads:
            add(r.w, raw_same)
        for w in writes:
            add(w.w, raw_same)
            for k, v in w.r.items():
                add((k, v), raw_same)
        wd = self.waited[eng]
        for k, v in need.items():
            if wd.get(k, 0) < v:
                wd[k] = v
                self.prog[eng].append(("w", k, v))
        self.cnt[semk] = self.cnt.get(semk, 0) + inc
        ev = (semk, self.cnt[semk])
        for f in fns[:-1]:
            self.prog[eng].append(("i", _record(f), None, 0))
        self.prog[eng].append(("i", _record(fns[-1]), semk, inc))
        for r in reads:
            if r.r.get(semk, 0) < ev[1]:
                r.r[semk] = ev[1]
        for w in writes:
            w.w = ev
            w.r = {}
        return ev

    def wait_final(self, eng, semk):
        v = self.cnt.get(semk, 0)
        if v and self.waited[eng].get(semk, 0) < v:
            self.waited[eng][semk] = v
            self.prog[eng].append(("w", semk, v))


class _Stop(Exception):
    pass


def build_nc(NB, stop=None):
    nc = bass.Bass("TRN2", target_bir_lowering=False)

    def ck(n):
        if stop == n:
            raise _Stop()
    sc = Sched()
    es = ExitStack()

    def din(name, shape, dt=F32):
        return nc.dram_tensor(name, list(shape), dt, kind="ExternalInput").ap()

    x_d = din("x", [NB, S, D])
    c_d = din("c", [NB, D])
    wada_d = din("w_ada", [D, 6 * D])
    bada_d = din("b_ada", [1, 6 * D])
    gmix_d = din("g_mix", [1, D])
    win_d = din("w_in", [D, INW])
    wpa_d = din("w_proj_attn", [512, D])
    gsgu_d = din("g_sgu", [1, 512])
    wsgu_d = din("w_sgu", [8, 128, 128])
    bsgu_d = din("b_sgu", [8, 128])
    wps_d = din("w_proj_sgu", [512, D])
    wout_d = din("w_out", [D, D])
    gffn_d = din("g_ffn", [1, D])
    wff1_d = din("w_ff1", [D, DFF])
    wff2_d = din("w_ff2", [DFF, D])
    gfin_d = din("g_final", [1, D])
    identf_d = din("c_ident", [128, 128])
    tri_d = din("c_tri", [128, 128])
    maskT_d = din("c_maskT", [128, 128])
    eoh_d = din("c_eoh", [8, 8 * 128])
    out_d = nc.dram_tensor("out", [NB, S, D], F32, kind="ExternalOutput").ap()

    NCH = 31
    wsc = nc.dram_tensor("wsc", [NCH, 128, 4096], BF16, kind="Internal").ap()
    wsc_res = [Res(f"wsc{i}") for i in range(NCH)]

    def sb(name, shape, dt):
        return es.enter_context(nc.sbuf_tensor(name, list(shape), dt))

    xsb = [sb(f"xs{i}", [128, NJ, D], F32) for i in range(2)]
    xs_res = [Res(f"xs{i}") for i in range(2)]
    junk = sb("junk", [128, D], BF16)
    xn = sb("xn", [128, NJ, D], BF16)
    hT = sb("hT", [128, 8, T], BF16)
    h2T = sb("h2T", [128, 8, T], BF16)
    Qp = sb("Qp", [128, 8, T], BF16)
    Kc = sb("Kc", [128, 4, S], BF16)
    Vc = sb("Vc", [128, 16, 8, 64], BF16)
    ones64 = sb("ones64", [128, 64], BF16)
    arena = sb("arena", [128, 8192], F32)
    arena_b = arena[:].bitcast(BF16)
    ares = [Res(f"ar{i}") for i in range(32)]
    rec = [sb(f"rec{i}", [64, 512], F32) for i in range(2)]
    tha = sb("tha", [128, T], F32)
    thb = sb("thb", [128, T], F32)
    tmp = [sb(f"tmp{i}", [128, 512], F32) for i in range(2)]
    sq = [sb(f"sq{i}", [128, T], F32) for i in range(2)]
    wsl = sb("wsl", [128, NSLOT * 2048], F32)
    identf = sb("identf", [128, 128], F32)
    identb = sb("identb", [128, 128], BF16)
    tri = sb("tri", [128, 128], BF16)
    maskT = sb("maskT", [128, 128], F32)
    eoh = sb("eoh", [128, 8, 128], BF16)
    wcT = sb("wcT", [128, 8, 128], BF16)
    bT = sb("bT", [128, 4, 128], F32)
    gsgu_bc = sb("gsgu_bc", [128, 512], F32)
    gfin_bc = sb("gfin_bc", [128, D], F32)
    g1h_bc = sb("g1h_bc", [128, D], F32)
    g2_bc = sb("g2_bc", [128, D], F32)
    gsc = nc.dram_tensor("gsc", [4, 2, D], F32, kind="Internal").ap()
    modc = [sb(f"modc{i}", [4, 256], F32) for i in range(2)]
    modT = sb("modT", [128, 48, 4], F32)
    gmT = sb("gmT", [128, 8], F32)
    gfT = sb("gfT", [128, 8], F32)
    A1 = sb("A1", [128, 8, 4], F32)
    A2 = sb("A2", [128, 8, 4], F32)
    cactT = sb("cactT", [128, 8, 4], F32)
    ssq = sb("ssq", [128, 16], F32)
    ksum = sb("ksum", [128, 4, 8], F32)
    ksb = sb("ksb", [128, 4, 8], BF16)
    Gs = sb("Gs", [128, 8, 8], F32)
    cmpt = sb("cmpt", [128, 8, 49], F32)
    rank = sb("rank", [128, 8, 8], F32)
    selb4 = [sb(f"selb{j}", [128, 8, 8], BF16) for j in range(NJ)]
    selb_res = [Res(f"selb{j}") for j in range(NJ)]
    selT = sb("selT", [128, 8, T], BF16)

    pbank = [es.enter_context(nc.psum_tensor(f"pb{i}", [128, 512], F32)) for i in range(8)]
    pres = [Res(f"pb{i}") for i in range(8)]
    prr = [0, 0]

    def pget(grp):
        i = grp * 4 + prr[grp] % 4
        prr[grp] += 1
        return pbank[i], pres[i]

    R = {n: Res(n) for n in (
        "junk", "xn", "hT", "h2T", "Qp", "Vc", "tha", "thb", "identf", "identb", "tri", "maskT", "eoh", "gsc",
        "wcT", "bT", "gsgu_bc", "gfin_bc", "g1h_bc", "g2_bc", "modT", "gmT", "gfT", "A1", "A2",
        "cactT", "ones64", "ssq", "ksum", "ksb", "Gs", "cmpt", "rank", "selT", "out")}
    Kc_res = [Res(f"Kc{i}") for i in range(NT)]
    mT_res = [Res(f"mT{i}") for i in range(8)]
    tmp_res = [Res(f"tmp{i}") for i in range(2)]
    sq_res = [Res(f"sq{i}") for i in range(2)]
    rec_res = [Res(f"rec{i}") for i in range(2)]
    modc_res = [Res(f"modc{i}") for i in range(2)]
    wres = [Res(f"wsl{i}") for i in range(NSLOT)]
    wrr = [0]

    def aT(ct):
        return arena_b[:, ct * 512:(ct + 1) * 512], [ares[ct]]

    arena_f = arena[:]
    bada_sb = arena_f[0:4, 1024:7168]
    bada_res = ares[4:28]
    cin = arena_f[0:4, 7168:8192]
    cact = cin
    cin_res = ares[28:32]

    def uT(ct):
        return arena_f[:, ct * 512:(ct + 1) * 512], [ares[2 * ct], ares[2 * ct + 1]]

    def vsg(j):
        return arena_f[:, 2048 + j * 512:2048 + (j + 1) * 512], [ares[8 + 2 * j], ares[9 + 2 * j]]

    def vn(j):
        return arena_b[:, 8192 + j * 512:8192 + (j + 1) * 512], [ares[16 + j]]

    def mT(ct):
        return arena_b[:, 4096 + ct * 512:4096 + (ct + 1) * 512], [ares[8 + ct]]

    def sT(gp):
        return arena_b[:, 10240 + gp * 512:10240 + (gp + 1) * 512], [ares[20 + gp]]

    def oT(pair):
        return arena_b[:, 12288 + pair * 512:12288 + (pair + 1) * 512], [ares[24 + pair]]

    def PT(i):
        return arena_b[:, 14336 + i * 512:14336 + (i + 1) * 512], [ares[28 + i]]

    def wslot_b(s):
        return wsl[:, s * 2048:(s + 1) * 2048].bitcast(BF16)

    dma_n = {"sp": 0, "pool": 0}
    NLANE = {"sp": 8, "pool": 24}
    lane_res = {e: [Res(f"{e}_lane{i}") for i in range(NLANE[e])] for e in NLANE}

    def dma(eng, out, in_, reads, writes, sem=None, ncdma=False, eng_override=None):
        if eng_override is not None:
            eng = eng_override
        if sem is None:
            lane = dma_n[eng] % NLANE[eng]
            dma_n[eng] += 1
            sem = f"{eng}_l{lane}"
            writes = list(writes) + [lane_res[eng][lane]]
        if ncdma:
            f = lambda e, o=out, i=in_: e.dma_start(out=o, in_=i, allow_slow_non_contiguous=True)
        else:
            f = lambda e, o=out, i=in_: e.dma_start(out=o, in_=i)
        return sc.op(eng, f, reads, writes, sem=sem, inc=16)

    try:
        dma("sp", identf[:], identf_d, [], [R["identf"]])
        dma("pool", identb[:], identf_d, [], [R["identb"]])
        dma("pool", tri[:], tri_d, [], [R["tri"]])
        dma("sp", maskT[:], maskT_d, [], [R["maskT"]])
        sc.op("dve", lambda e: e.memset(ones64[:], 1.0), [], [R["ones64"]])
        sc.op("dve", lambda e: e.memset(eoh[:], 0.0), [], [R["eoh"]])
        sc.op("dve", lambda e: e.memset(selT[:], 0.0), [], [R["selT"]])
        sc.op("dve", lambda e: e.memset(Qp[:], 0.0), [], [R["Qp"]])
        dma("pool", eoh[0:8], eoh_d.rearrange("j (k n) -> j k n", k=8), [], [R["eoh"]])
        dma("pool", eoh[64:72], eoh_d.rearrange("j (k n) -> j k n", k=8), [], [R["eoh"]])
        dma("sp", cin[0:NB, :], c_d, [], cin_res)
        dma("sp", bada_sb, bada_d[0:1, :].to_broadcast([4, 6 * D]), [], bada_res)
        dma("sp", gsgu_bc[:], gsgu_d[0:1, :].to_broadcast([128, 512]), [], [R["gsgu_bc"]])
        dma("sp", gfin_bc[:], gfin_d[0:1, :].to_broadcast([128, D]), [], [R["gfin_bc"]])
        dma("sp", gmT[:], gmix_d[0, :].rearrange("(k p) -> p k", p=128), [], [R["gmT"]], ncdma=True)
        dma("sp", gfT[:], gffn_d[0, :].rearrange("(k p) -> p k", p=128), [], [R["gfT"]], ncdma=True)
        for g in range(8):
            dma("sp", bT[(g % 2) * 64:(g % 2) * 64 + 64, g // 2, :], bsgu_d[g:g + 1, :].to_broadcast([64, 128]), [], [R["bT"]])

        def conv(ci, pieces):
            evs = []
            for (c0, ncols, src, nkt) in pieces:
                o = wsc[ci, :, c0:c0 + nkt * ncols].rearrange("p (k n) -> p k n", k=nkt)
                i = src.rearrange("(k p) n -> p k n", p=128)
                evs.append((o, i))
            for n, (o, i) in enumerate(evs):
                dma("pool", o, i, [], [wsc_res[ci]] if n == len(evs) - 1 else [])

        def conv_multi(ci, pieces):
            subs = []
            for n, (c0, ncols, src, nkt) in enumerate(pieces):
                o = wsc[ci, :, c0:c0 + nkt * ncols].rearrange("p (k n) -> p k n", k=nkt)
                i = src.rearrange("(k p) n -> p k n", p=128)
                r = Res(f"wsc{ci}_{n}")
                dma("pool", o, i, [R["modT"]] if ci >= 5 else [], [r])
                subs.append(r)
            return subs

        wsc_sub = {}
        for q in range(5):
            wsc_sub[q] = conv_multi(q, [(0, 512, win_d[:, q * 512:(q + 1) * 512], 8)])
        ck(1)

        sc.op("act", lambda e: e.activation(out=cact[0:NB, :], in_=cin[0:NB, :], func=AF.Silu), cin_res, cin_res)
        pb, pr = pget(0)
        sc.op("pe", [(lambda e, k=k: e.transpose(out=pb[:, k * 4:k * 4 + NB], in_=cact[0:NB, k * 128:(k + 1) * 128],
                                                 identity=identf[0:NB, 0:NB])) for k in range(8)],
              cin_res + [R["identf"]], [pr])
        sc.op("dve", lambda e: e.memset(cactT[:], 0.0), [], [R["cactT"]])
        sc.op("dve", lambda e: e.tensor_copy(out=cactT[:, :, 0:NB], in_=pb[:, 0:32].rearrange("p (k b) -> p k b", b=4)[:, :, 0:NB]),
              [pr], [R["cactT"]])

        NCC = 24
        for cc in range(NCC):
            s = wrr[0] % NSLOT
            wrr[0] += 1
            wv = wsl[:, s * 2048:(s + 1) * 2048].rearrange("p (k n) -> p k n", k=8)
            dma("sp", wv, wada_d[:, cc * 256:(cc + 1) * 256].rearrange("(k p) n -> p k n", p=128), [], [wres[s]], sem=f"w{s}")
            pb, pr = pget(0)
            sc.op("pe", [(lambda e, k=k, wv=wv, pb=pb: e.matmul(pb[0:4, 0:256], lhsT=cactT[:, k, :], rhs=wv[:, k, :],
                                                              start=(k == 0), stop=(k == 7))) for k in range(8)],
                  [wres[s], R["cactT"]], [pr])
            mi = cc % 2
            sc.op("dve", lambda e, pb=pb, mi=mi, cc=cc: e.tensor_tensor(out=modc[mi][:, 0:256], in0=pb[0:4, 0:256],
                                                                        in1=bada_sb[:, cc * 256:(cc + 1) * 256], op=ALU.add),
                  [pr] + bada_res, [modc_res[mi]])
            col = cc * 256
            if 2048 <= col < 3072 or 5120 <= col < 6144:
                gi = 0 if col < 3072 else 1
                off = col - (2048 if gi == 0 else 5120)
                dma("sp", gsc[:, gi, off:off + 256], modc[mi][:, 0:256], [modc_res[mi]], [R["gsc"]])
            pb2, pr2 = pget(1)
            sc.op("pe", [(lambda e, q=q, mi=mi, pb2=pb2: e.transpose(out=pb2[:, q * 4:q * 4 + 4], in_=modc[mi][:, q * 128:(q + 1) * 128],
                                                                    identity=identf[0:4, 0:4])) for q in range(2)],
                  [modc_res[mi], R["identf"]], [pr2])
            sc.op("act", lambda e, cc=cc, pb2=pb2: e.activation(out=modT[:, cc * 2:cc * 2 + 2, :],
                                                               in_=pb2[:, 0:8].rearrange("p (q b) -> p q b", b=4), func=AF.Copy),
                  [pr2], [R["modT"]])
        ck(2)

        for ct in range(8):
            wsc_sub[5 + ct] = conv_multi(5 + ct, [
                (0, 128, win_d[:, 2560 + ct * 128:2560 + (ct + 1) * 128], 8),
                (1024, 128, win_d[:, 3584 + ct * 128:3584 + (ct + 1) * 128], 8),
                (2048, 128, wpa_d[:, ct * 128:(ct + 1) * 128], 4),
                (2560, 128, wps_d[:, ct * 128:(ct + 1) * 128], 4)])
        for dh in range(2):
            wsc_sub[13 + dh] = conv_multi(13 + dh, [(0, 512, wout_d[:, dh * 512:(dh + 1) * 512], 8)])
        for c8 in range(8):
            wsc_sub[15 + c8] = conv_multi(15 + c8, [(0, 512, wff1_d[:, c8 * 512:(c8 + 1) * 512], 8)])
        for dh in range(2):
            for kc in range(4):
                wsc_sub[23 + dh * 4 + kc] = conv_multi(23 + dh * 4 + kc,
                                                       [(0, 512, wff2_d[kc * 1024:(kc + 1) * 1024, dh * 512:(dh + 1) * 512], 8)])

        sc.op("dve", lambda e: e.scalar_tensor_tensor(out=A1[:], in0=modT[:, 8:16, :], scalar=1.0,
                                                      in1=gmT[:].unsqueeze(2).to_broadcast([128, 8, 4]), op0=ALU.add, op1=ALU.mult),
              [R["modT"], R["gmT"]], [R["A1"]])
        sc.op("dve", lambda e: e.scalar_tensor_tensor(out=A2[:], in0=modT[:, 32:40, :], scalar=1.0,
                                                      in1=gfT[:].unsqueeze(2).to_broadcast([128, 8, 4]), op0=ALU.add, op1=ALU.mult),
              [R["modT"], R["gfT"]], [R["A2"]])

        wtmp = arena_f[:, 0:1024].rearrange("p (g s) -> p g s", g=8)
        wtmp_res = [ares[0], ares[1], ares[2], ares[3]]
        dma("sp", wtmp, wsgu_d.rearrange("g t s -> t g s"), [], wtmp_res)
        for half in range(2):
            pb, pr = pget(0)
            sc.op("pe", [(lambda e, g=g, pb=pb, half=half: e.transpose(out=pb[:, g * 128:(g + 1) * 128], in_=wtmp[:, half * 4 + g, :],
                                                                      identity=identf[:])) for g in range(4)],
                  wtmp_res + [R["identf"]], [pr])
            sc.op("dve", lambda e, pb=pb, half=half: e.tensor_tensor(
                out=wcT[:, half * 4:half * 4 + 4, :], in0=pb[:, 0:512].rearrange("p (g t) -> p g t", g=4),
                in1=maskT[:].unsqueeze(1).to_broadcast([128, 4, 128]), op=ALU.mult),
                [pr, R["maskT"]], [R["wcT"]])

        sc.op("dve", lambda e: e.memset(ksum[:], 0.0), [], [R["ksum"]])
        sc.op("dve", lambda e: e.memset(ksb[:], 0.0), [], [R["ksb"]])
        ck(3)

        def load_w(ci, n=4096):
            s = wrr[0] % NSLOT
            wrr[0] += 1
            dma("sp", wslot_b(s)[:, 0:n], wsc[ci, :, 0:n], wsc_sub[ci], [wres[s]], sem=f"w{s}")
            return wslot_b(s), wres[s]

        def stats_begin(col0, n=NJ):
            sc.op("dve", lambda e: e.memset(ssq[:, col0:col0 + n], 0.0), [], [R["ssq"]])

        def stats_row(col0, j, src, src_res, width):
            sc.op("act", lambda e: e.activation(out=junk[:, 0:width], in_=src, func=AF.Square,
                                                accum_out=ssq[:, col0 + j:col0 + j + 1]),
                  list(src_res) + [R["ssq"]], [R["junk"], R["ssq"]])

        def rms_stats(col0, src_fn, src_res, width, n=NJ, rows=True):
            if rows:
                stats_begin(col0, n)
                for j in range(n):
                    stats_row(col0, j, src_fn(j), src_res(j), width)
            rstd_cols(col0, n, width)

        def rstd_cols(col0, n, width):
            sc.op("dve", lambda e: e.tensor_scalar(out=ssq[:, col0:col0 + n], in0=ssq[:, col0:col0 + n], scalar1=1.0 / width,
                                                   scalar2=EPS, op0=ALU.mult, op1=ALU.add), [R["ssq"]], [R["ssq"]])
            sc.op("act", lambda e: e.activation(out=ssq[:, col0:col0 + n], in_=ssq[:, col0:col0 + n], func=AF.Sqrt),
                  [R["ssq"]], [R["ssq"]])
            sc.op("dve", lambda e: e.reciprocal(out=ssq[:, col0:col0 + n], in_=ssq[:, col0:col0 + n]), [R["ssq"]], [R["ssq"]])

        xn2 = arena_b[:, 0:4096].rearrange("p (j d) -> p j d", j=NJ)

        def xn_of(second):
            if second:
                return xn2, (lambda j: [ares[2 * j], ares[2 * j + 1]]), ares[0:8]
            return xn, (lambda j: [R["xn"]]), [R["xn"]]

        def norm_xn(col0, xs, Rxs, rows=None, second=False):
            xb_, xr_, _ = xn_of(second)
            for j in (range(NJ) if rows is None else rows):
                sc.op("dve", lambda e, j=j: e.tensor_scalar(out=xb_[:, j, :], in0=xs[:, j, :], scalar1=ssq[:, col0 + j:col0 + j + 1],
                                                            scalar2=None, op0=ALU.mult), [Rxs, R["ssq"]], xr_(j))

        def norm_to_T(col0, dstT, dst_res, Amod, shcol, b, xs, Rxs, do_xn=True, second=False, evac="act"):
            if do_xn:
                norm_xn(col0, xs, Rxs, second=second)
            xb_, _, xall = xn_of(second)
            for k in range(8):
                pb, pr = pget(0)
                pbb = pb[:].bitcast(BF16)
                sc.op("pe", [(lambda e, j=j, k=k, pbb=pbb: e.transpose(out=pbb[:, j * 128:(j + 1) * 128],
                                                                      in_=xb_[:, j, k * 128:(k + 1) * 128], identity=identb[:]))
                             for j in range(NJ)], list(xall) + [R["identb"]], [pr])
                if evac == "act":
                    sc.op("act", lambda e, k=k, pbb=pbb: e.activation(out=dstT[:, k, :], in_=pbb[:, 0:T], func=AF.Identity,
                                                                     scale=Amod[:, k, b:b + 1], bias=modT[:, shcol + k, b:b + 1]),
                          [pr, R["A1"], R["A2"], R["modT"]], [dst_res])
                else:
                    sc.op("dve", lambda e, k=k, pbb=pbb: e.tensor_scalar(out=dstT[:, k, :], in0=pbb[:, 0:T],
                                                                        scalar1=Amod[:, k, b:b + 1], scalar2=modT[:, shcol + k, b:b + 1],
                                                                        op0=ALU.mult, op1=ALU.add),
                          [pr, R["A1"], R["A2"], R["modT"]], [dst_res])

        def mm_feat(w, wr, ct, src, src_res):
            pb, pr = pget(0)
            sc.op("pe", [(lambda e, k=k, pb=pb: e.matmul(pb[:, :], lhsT=w[:, k, ct * 128:(ct + 1) * 128], rhs=src[:, k, :],
                                                       start=(k == 0), stop=(k == 7))) for k in range(8)],
                  [wr, src_res], [pr])
            return pb, pr

        def mm_tok(w, wr, j, src, src_res, grp=0):
            pb, pr = pget(grp)
            sc.op("pe", [(lambda e, k=k, pb=pb: e.matmul(pb[:, :], lhsT=src[:, k, j * 128:(j + 1) * 128], rhs=w[:, k, :],
                                                       start=(k == 0), stop=(k == 7))) for k in range(8)],
                  [wr, src_res], [pr])
            return pb, pr

        tiles = [(bb, tt) for bb in range(NB) for tt in range(NT)]

        def prefetch_A(idx):
            bb, tt = tiles[idx]
            p = idx % 2
            xs_, Rx = xsb[p], xs_res[p]
            dma("sp", xs_[:], x_d[bb, tt * T:(tt + 1) * T, :].rearrange("(j p) d -> p j d", p=128), [], [Rx], sem=f"x_ld{p}")
            rms_stats(0, lambda j: xs_[:, j, :], lambda j: [Rx], D)
            norm_xn(0, xs_, Rx)

        def prefetch_B(idx):
            bb, tt = tiles[idx]
            p = idx % 2
            norm_to_T(0, hT, R["hT"], A1, 0, bb, xsb[p], xs_res[p], do_xn=False, evac=("act" if idx == 0 else "dve"))

        prefetch_A(0)
        prefetch_B(0)
        for idx, (b, t) in enumerate(tiles):
            if True:
                tok0 = t * T
                xs, Rxs = xsb[idx % 2], xs_res[idx % 2]
                if t == 0:
                    gq = "sp" if b == 0 else "pool"
                    dma(gq, g1h_bc[:], gsc[b:b + 1, 0, :].to_broadcast([128, D]), [R["gsc"]], [R["g1h_bc"]])
                    dma(gq, g2_bc[:], gsc[b:b + 1, 1, :].to_broadcast([128, D]), [R["gsc"]], [R["g2_bc"]])
                ck(t * 10 + 4)

                w, wr = load_w(4)
                w = w.rearrange("p (k n) -> p k n", k=8)
                for j in range(NJ):
                    pb, pr = mm_tok(w, wr, j, hT, R["hT"])
                    va, vr = vsg(j)
                    sc.op("act", lambda e, pb=pb, va=va: e.activation(out=va, in_=pb[:, :], func=AF.Gelu_apprx_tanh), [pr], vr)
                rms_stats(4, lambda j: vsg(j)[0], lambda j: vsg(j)[1], 512)
                for j in range(NJ):
                    va, vr = vsg(j)
                    na, nr = vn(j)
                    sc.op("dve", lambda e, va=va, na=na, j=j: e.scalar_tensor_tensor(out=na, in0=va, scalar=ssq[:, 4 + j:5 + j],
                                                                                    in1=gsgu_bc[:], op0=ALU.mult, op1=ALU.mult),
                          vr + [R["ssq"], R["gsgu_bc"]], nr)
                w, wr = load_w(3)
                w = w.rearrange("p (k n) -> p k n", k=8)
                for ct in range(4):
                    pb, pr = mm_feat(w, wr, ct, hT, R["hT"])
                    ua, ur = uT(ct)
                    sc.op("act", lambda e, pb=pb, ua=ua: e.activation(out=ua, in_=pb[:, :], func=AF.Gelu_apprx_tanh), [pr], ur)
                w, wr = load_w(0)
                w = w.rearrange("p (k n) -> p k n", k=8)
                for ct in range(4):
                    pb, pr = mm_feat(w, wr, ct, hT, R["hT"])
                    for par in range(2):
                        sc.op("dve", lambda e, pb=pb, ct=ct, par=par: e.tensor_scalar(
                            out=Qp[par * 64:(par + 1) * 64, 2 * ct + par, :], in0=pb[par * 64:(par + 1) * 64, :], scalar1=0.125,
                            scalar2=None, op0=ALU.mult), [pr], [R["Qp"]])
                w, wr = load_w(1)
                w = w.rearrange("p (k n) -> p k n", k=8)
                sc.op("dve", lambda e: e.memset(ksum[:, :, 2 * t:2 * t + 2], 0.0), [], [R["ksum"]])
                for ct in range(4):
                    pb, pr = mm_feat(w, wr, ct, hT, R["hT"])
                    sc.op("act", [(lambda e, pb=pb, ct=ct, bl=bl: e.activation(
                        out=Kc[:, ct, tok0 + bl * 256:tok0 + (bl + 1) * 256], in_=pb[:, bl * 256:(bl + 1) * 256], func=AF.Copy,
                        accum_out=ksum[:, ct, 2 * t + bl:2 * t + bl + 1])) for bl in range(2)],
                          [pr, R["ksum"]], [Kc_res[t], R["ksum"]])
                sc.op("dve", lambda e: e.tensor_copy(out=ksb[:, :, 2 * t:2 * t + 2], in_=ksum[:, :, 2 * t:2 * t + 2]),
                      [R["ksum"]], [R["ksb"]])
                ck(t * 10 + 5)
                use_sel = (t >= 2) and USE_SEL
                if use_sel:
                    gbank = []
                    for j in range(NJ):
                        qb = 2 * t + j // 2
                        pb, pr = pget(1)
                        gbank.append((pb, pr))
                        sc.op("pe", [(lambda e, pb=pb, h=h, j=j, qb=qb: e.matmul(
                            pb[:, h * 8:h * 8 + qb], lhsT=Qp[:, h, j * 128:(j + 1) * 128],
                            rhs=ksb[:, h // 2, 0:qb], start=True, stop=True)) for h in range(H)],
                              [R["Qp"], R["ksb"]], [pr])
                    for j in range(NJ):
                        qb = 2 * t + j // 2
                        pb, pr = gbank[j]
                        sc.op("dve", lambda e, pb=pb, qb=qb: e.tensor_copy(
                            out=Gs[:, :, 0:qb], in_=pb[:, 0:64].rearrange("p (h n) -> p h n", n=8)[:, :, 0:qb]), [pr], [R["Gs"]])
                        sc.op("dve", lambda e, qb=qb: e.tensor_tensor(
                            out=cmpt[:, :, 0:qb * qb].rearrange("p h (a m) -> p h a m", m=qb),
                            in0=Gs[:, :, 0:qb].unsqueeze(2).to_broadcast([128, 8, qb, qb]),
                            in1=Gs[:, :, 0:qb].unsqueeze(3).to_broadcast([128, 8, qb, qb]), op=ALU.is_gt),
                            [R["Gs"]], [R["cmpt"]])
                        sc.op("dve", lambda e, qb=qb: e.reduce_sum(out=rank[:, :, 0:qb],
                                                                  in_=cmpt[:, :, 0:qb * qb].rearrange("p h (a m) -> p h a m", m=qb),
                                                                  axis=AX.X), [R["cmpt"]], [R["rank"]])
                        sc.op("dve", lambda e, j=j: e.memset(selb4[j][:], 0.0), [], [selb_res[j]])
                        sc.op("dve", lambda e, qb=qb, j=j: e.tensor_scalar(out=selb4[j][:, :, 0:qb], in0=rank[:, :, 0:qb], scalar1=2.5,
                                                                          scalar2=NEG, op0=ALU.is_ge, op1=ALU.mult),
                              [R["rank"], selb_res[j]], [selb_res[j]])

                w, wr = load_w(2)
                w = w.rearrange("p (k n) -> p k n", k=8)
                for j in range(NJ):
                    pb, pr = mm_tok(w, wr, j, hT, R["hT"])
                    sc.op("dve", lambda e, pb=pb, j=j: e.tensor_copy(out=Vc[:, 4 * t + j, 0:8, :],
                                                                    in_=pb[:, :].rearrange("p (h d) -> p h d", d=64)),
                          [pr], [R["Vc"]])
                for gp in range(4):
                    pb, pr = pget(1)
                    fns = []
                    rds = [R["wcT"]]
                    for j in range(NJ):
                        na, nr = vn(j)
                        rds += nr
                        for hf in range(2):
                            g = 2 * gp + hf
                            fns.append(lambda e, pb=pb, na=na, j=j, hf=hf, g=g: e.matmul(
                                pb[hf * 64:(hf + 1) * 64, j * 128:(j + 1) * 128], lhsT=na[:, g * 64:(g + 1) * 64], rhs=wcT[:, g, :],
                                start=True, stop=True, tile_position=(0, hf * 64)))
                    sc.op("pe", fns, rds, [pr])
                    sa, sr = sT(gp)
                    ua, ur = uT(gp)
                    t0, t0r = tmp[gp % 2], tmp_res[gp % 2]
                    sc.op("dve", lambda e, pb=pb, t0=t0, gp=gp: e.tensor_tensor(
                        out=t0[:, :].rearrange("p (j t) -> p j t", t=128), in0=pb[:, :].rearrange("p (j t) -> p j t", t=128),
                        in1=bT[:, gp:gp + 1, :].to_broadcast([128, 4, 128]), op=ALU.add), [pr, R["bT"]], [t0r])
                    sc.op("dve", lambda e, t0=t0, sa=sa, ua=ua: e.tensor_tensor(out=sa, in0=t0[:, :], in1=ua, op=ALU.mult),
                          [t0r] + ur, sr)

                if use_sel:
                    for j in range(NJ):
                        pb2, pr2 = pget(1)
                        sc.op("pe", [(lambda e, h=h, pb2=pb2, j=j: e.matmul(
                            pb2[(h % 2) * 64:(h % 2) * 64 + 8, (h // 2) * 128:(h // 2 + 1) * 128], lhsT=selb4[j][:, h, :], rhs=identb[:],
                            start=True, stop=True, tile_position=(0, (h % 2) * 64))) for h in range(H)],
                              [selb_res[j], R["identb"]], [pr2])
                        for par in range(2):
                            sc.op("dve", lambda e, j=j, pb2=pb2, par=par: e.tensor_copy(
                                out=selT[par * 64:par * 64 + 8, par::2, j * 128:(j + 1) * 128],
                                in_=pb2[par * 64:par * 64 + 8, :].rearrange("p (h q) -> p h q", h=4)),
                                  [pr2], [R["selT"]])

                steps = []
                for h in range(H):
                    ents = []
                    for kt in range(4 * t):
                        ents.append((kt, 0, 512, None, (0, 512) if use_sel else None))
                    ents.append((4 * t, 0, 512, (0, 128), (256, 512) if use_sel else None))
                    ents.append((4 * t + 1, 128, 512, (128, 256), (256, 512) if use_sel else None))
                    ents.append((4 * t + 2, 256, 512, (256, 384), None))
                    ents.append((4 * t + 3, 384, 512, (384, 512), None))
                    for i, en in enumerate(ents):
                        steps.append((h, i, len(ents), en))
                acc_of = {}

                def emit_scores(si):
                    h, i, n, (kt, c0, c1, tric, biasc) = steps[si]
                    off, pair = (h % 2) * 64, h // 2
                    pb, pr = pget(0)
                    fns = []
                    rds = [R["Qp"], Kc_res[kt // 4]]
                    extra = (tric is not None) or (biasc is not None and SEL_APPLY)
                    nextra = (1 if tric is not None else 0) + (1 if (biasc is not None and SEL_APPLY) else 0)
                    fns.append(lambda e: e.matmul(pb[:, c0:c1], lhsT=Kc[:, pair, kt * 128:(kt + 1) * 128],
                                                  rhs=Qp[:, h, c0:c1], start=True, stop=not extra))
                    if biasc is not None and SEL_APPLY:
                        nextra -= 1
                        rds += [R["eoh"], R["selT"]]
                        fns.append(lambda e, last=(nextra == 0): e.matmul(
                            pb[:, biasc[0]:biasc[1]], lhsT=eoh[:, kt // 2, :],
                            rhs=selT[:, h, biasc[0]:biasc[1]], start=False, stop=last))
                    if tric is not None:
                        rds += [R["identb"], R["tri"]]
                        fns.append(lambda e: e.matmul(pb[:, tric[0]:tric[1]], lhsT=identb[:], rhs=tri[:], start=False, stop=True))
                    sc.op("pe", fns, rds, [pr])
                    return pb, pr

                def emit_exp_pv(si, pb, pr, pslot):
                    h, i, n, (kt, c0, c1, tric, biasc) = steps[si]
                    off, pair = (h % 2) * 64, h // 2
                    pa, par = PT(pslot)
                    sc.op("act", lambda e: e.activation(out=pa[:, c0:c1], in_=pb[:, c0:c1], func=AF.Exp), [pr], par)
                    if i == 0:
                        acc_of[h] = pget(1)
                    ab, ar = acc_of[h]
                    first, last = (i == 0), (i == n - 1)
                    sc.op("pe", [
                        lambda e: e.matmul(ab[0:64, c0:c1], lhsT=Vc[:, kt, h, :], rhs=pa[:, c0:c1], start=first, stop=last,
                                           tile_position=(0, 0)),
                        lambda e: e.matmul(ab[64:128, c0:c1], lhsT=ones64[:, :], rhs=pa[:, c0:c1], start=first, stop=last,
                                           tile_position=(0, 64))],
                          par + [R["Vc"], R["ones64"]], [ar])
                    if last:
                        ri = h % 2
                        sc.op("dve", lambda e: e.reciprocal(out=rec[ri][:, :], in_=ab[64:128, :]), [ar], [rec_res[ri]])
                        oa, orr = oT(pair)
                        sc.op("dve", lambda e: e.tensor_tensor(out=oa[off:off + 64, :], in0=ab[0:64, :], in1=rec[ri][:, :],
                                                               op=ALU.mult), [ar, rec_res[ri]], orr)

                nst = len(steps)
                LOOK = 1
                pend = [emit_scores(si) for si in range(min(LOOK, nst))]
                for si in range(nst):
                    if si + LOOK < nst:
                        pend.append(emit_scores(si + LOOK))
                    pb_, pr_ = pend.pop(0)
                    emit_exp_pv(si, pb_, pr_, si % 4)

                if idx + 1 < len(tiles):
                    prefetch_A(idx + 1)
                ck(t * 10 + 6)
                for ct in range(8):
                    w, wr = load_w(5 + ct, 3072)
                    wga = w[:, 0:1024].rearrange("p (k n) -> p k n", k=8)
                    wgb = w[:, 1024:2048].rearrange("p (k n) -> p k n", k=8)
                    wpa = w[:, 2048:2560].rearrange("p (k n) -> p k n", k=4)
                    wps = w[:, 2560:3072].rearrange("p (k n) -> p k n", k=4)
                    grp = ct % 2
                    pC, pCr = pget(grp)
                    sc.op("pe", [(lambda e, k=k, pC=pC, wga=wga: e.matmul(pC[:, :], lhsT=wga[:, k, :], rhs=hT[:, k, :], start=(k == 0), stop=(k == 7)))
                                 for k in range(8)], [wr, R["hT"]], [pCr])
                    pD, pDr = pget(grp)
                    sc.op("pe", [(lambda e, k=k, pD=pD, wgb=wgb: e.matmul(pD[:, :], lhsT=wgb[:, k, :], rhs=hT[:, k, :], start=(k == 0), stop=(k == 7)))
                                 for k in range(8)], [wr, R["hT"]], [pDr])
                    pA, pAr = pget(grp)
                    rds = [wr]
                    for k in range(4):
                        rds += oT(k)[1]
                    sc.op("pe", [(lambda e, k=k, pA=pA, wpa=wpa: e.matmul(pA[:, :], lhsT=wpa[:, k, :], rhs=oT(k)[0], start=(k == 0), stop=(k == 3)))
                                 for k in range(4)], rds, [pAr])
                    pB, pBr = pget(grp)
                    rds = [wr]
                    for k in range(4):
                        rds += sT(k)[1]
                    sc.op("pe", [(lambda e, k=k, pB=pB, wps=wps: e.matmul(pB[:, :], lhsT=wps[:, k, :], rhs=sT(k)[0], start=(k == 0), stop=(k == 3)))
                                 for k in range(4)], rds, [pBr])
                    sc.op("act", lambda e, pC=pC: e.activation(out=tha[:, :], in_=pC[:, :], func=AF.Tanh, scale=0.5), [pCr], [R["tha"]])
                    sc.op("act", lambda e, pD=pD: e.activation(out=thb[:, :], in_=pD[:, :], func=AF.Tanh, scale=0.5), [pDr], [R["thb"]])
                    sc.op("dve", lambda e, pA=pA: e.scalar_tensor_tensor(out=tha[:, :], in0=tha[:, :], scalar=1.0, in1=pA[:, :],
                                                                        op0=ALU.add, op1=ALU.mult), [R["tha"], pAr], [R["tha"]])
                    sc.op("dve", lambda e, pB=pB: e.scalar_tensor_tensor(out=thb[:, :], in0=thb[:, :], scalar=1.0, in1=pB[:, :],
                                                                        op0=ALU.add, op1=ALU.mult), [R["thb"], pBr], [R["thb"]])
                    ma, mr = mT(ct)
                    sc.op("dve", lambda e, ma=ma: e.tensor_tensor(out=ma, in0=tha[:, :], in1=thb[:, :], op=ALU.add),
                          [R["tha"], R["thb"]], mr)

                wo = []
                for dh in range(2):
                    w, wr = load_w(13 + dh)
                    wo.append((w.rearrange("p (k n) -> p k n", k=8), wr))
                stats_begin(8)
                for j in range(NJ):
                    for dh in range(2):
                        w, wr = wo[dh]
                        pb, pr = pget(1)
                        sc.op("pe", [(lambda e, k=k, pb=pb, w=w, j=j: e.matmul(pb[:, :], lhsT=mT(k)[0][:, j * 128:(j + 1) * 128], rhs=w[:, k, :],
                                                                             start=(k == 0), stop=(k == 7))) for k in range(8)],
                              [wr] + [r for k in range(8) for r in mT(k)[1]], [pr])
                        ti = (dh * 4 + j) % 2
                        sc.op("dve", lambda e, pb=pb, ti=ti, dh=dh: e.scalar_tensor_tensor(
                            out=tmp[ti][:, :], in0=pb[:, :], scalar=0.5, in1=g1h_bc[:, dh * 512:(dh + 1) * 512],
                            op0=ALU.mult, op1=ALU.mult), [pr, R["g1h_bc"]], [tmp_res[ti]])
                        sc.op("dve", lambda e, ti=ti, j=j, dh=dh: e.tensor_tensor(out=xs[:, j, dh * 512:(dh + 1) * 512],
                                                                                  in0=xs[:, j, dh * 512:(dh + 1) * 512], in1=tmp[ti][:, :],
                                                                                  op=ALU.add), [tmp_res[ti], Rxs], [Rxs])
                    stats_row(8, j, xs[:, j, :], [Rxs], D)
                    rstd_cols(8 + j, 1, D)
                    norm_xn(8, xs, Rxs, rows=[j], second=True)

                ck(t * 10 + 7)
                if idx + 1 < len(tiles):
                    prefetch_B(idx + 1)
                norm_to_T(8, h2T, R["h2T"], A2, 24, b, xs, Rxs, do_xn=False, second=True)
                for c8 in range(8):
                    w, wr = load_w(15 + c8)
                    w = w.rearrange("p (k n) -> p k n", k=8)
                    for ct in range(4):
                        pb, pr = mm_feat(w, wr, ct, h2T, R["h2T"])
                        qi = ct % 2
                        aa, aar = aT(c8 * 4 + ct)
                        sc.op("act", lambda e, pb=pb, qi=qi: e.activation(out=sq[qi][:, :], in_=pb[:, :], func=AF.Square), [pr], [sq_res[qi]])
                        sc.op("dve", lambda e, pb=pb, qi=qi, aa=aa: e.scalar_tensor_tensor(out=aa, in0=pb[:, :], scalar=0.0, in1=sq[qi][:, :],
                                                                                          op0=ALU.is_gt, op1=ALU.mult),
                              [pr, sq_res[qi]], aar)
                for dh in range(2):
                    grp = 1 - dh
                    banks = [pget(grp) for _ in range(NJ)]
                    for kc in range(4):
                        w, wr = load_w(23 + dh * 4 + kc)
                        w = w.rearrange("p (k n) -> p k n", k=8)
                        for j in range(NJ):
                            pb, pr = banks[j]
                            rds = [wr]
                            for k in range(8):
                                rds += aT(kc * 8 + k)[1]
                            sc.op("pe", [(lambda e, k=k, pb=pb, w=w, j=j, kc=kc: e.matmul(
                                pb[:, :], lhsT=aT(kc * 8 + k)[0][:, j * 128:(j + 1) * 128], rhs=w[:, k, :],
                                start=(kc == 0 and k == 0), stop=(kc == 3 and k == 7))) for k in range(8)], rds, [pr])
                    for j in range(NJ):
                        pb, pr = banks[j]
                        ti = (dh * 4 + j) % 2
                        sc.op("dve", lambda e, pb=pb, ti=ti, dh=dh: e.tensor_tensor(out=tmp[ti][:, :], in0=pb[:, :],
                                                                                   in1=g2_bc[:, dh * 512:(dh + 1) * 512], op=ALU.mult),
                              [pr, R["g2_bc"]], [tmp_res[ti]])
                        sc.op("dve", lambda e, ti=ti, j=j, dh=dh: e.tensor_tensor(out=xs[:, j, dh * 512:(dh + 1) * 512],
                                                                                  in0=xs[:, j, dh * 512:(dh + 1) * 512], in1=tmp[ti][:, :],
                                                                                  op=ALU.add), [tmp_res[ti], Rxs], [Rxs])

                ck(t * 10 + 8)
                rms_stats(12, lambda j: xs[:, j, :], lambda j: [Rxs], D)
                for j in range(NJ):
                    sc.op("dve", lambda e, j=j: e.scalar_tensor_tensor(out=xs[:, j, :], in0=xs[:, j, :], scalar=ssq[:, 12 + j:13 + j],
                                                                      in1=gfin_bc[:], op0=ALU.mult, op1=ALU.mult),
                          [Rxs, R["ssq"], R["gfin_bc"]], [Rxs])
                dma("sp", out_d[b, tok0:tok0 + T, :].rearrange("(j p) d -> p j d", p=128), xs[:], [Rxs], [R["out"]], sem=f"out_st{idx % 2}", eng_override="pool")

    except _Stop:
        pass
    for semk in list(sc.cnt.keys()):
        sc.wait_final("sp", semk)

    sems = {k: es.enter_context(nc.semaphore(k)) for k in sc.cnt}

    def replay(items, eng):
        for it in items:
            if it[0] == "w":
                eng.wait_ge(sems[it[1]], it[2])
            else:
                name, a, k = it[1]
                ins = getattr(eng, name)(*a, **k)
                if it[2] is not None:
                    ins.then_inc(sems[it[2]], it[3])

    with nc.Block() as block:
        @block.tensor
        def _(e):
            replay(sc.prog["pe"], e)

        @block.scalar
        def _(e):
            replay(sc.prog["act"], e)

        @block.vector
        def _(e):
            replay(sc.prog["dve"], e)

        @block.gpsimd
        def _(e):
            replay(sc.prog["pool"], e)

        @block.sync
        def _(e):
            replay(sc.prog["sp"], e)
    es.close()
    return nc


def make_consts(NB):
    ident = np.eye(128, dtype=np.float32)
    ki = np.arange(128)[:, None]
    qi = np.arange(128)[None, :]
    tri = np.where(ki <= qi, 0.0, NEG).astype(np.float32)
    maskT = (qi >= ki).astype(np.float32)
    eoh = np.zeros((8, 8, 128), np.float32)
    for j in range(8):
        eoh[j, j, :] = 1.0
    return {"c_ident": ident, "c_tri": tri, "c_maskT": maskT, "c_eoh": eoh.reshape(8, 1024)}


_NC_CACHE = {}


def kernel(x, c, w_ada, b_ada, g_mix, w_in, w_proj_attn, g_sgu, w_sgu, b_sgu, w_proj_sgu, w_out, g_ffn, w_ff1, w_ff2,
           g_final):
    f = lambda a: np.ascontiguousarray(np.asarray(a, dtype=np.float32))
    x = f(x)
    c = f(c)
    B = x.shape[0]
    NB = B // NCORES
    shared = {
        "w_ada": f(w_ada)[0], "b_ada": f(b_ada).reshape(1, -1), "g_mix": f(g_mix).reshape(1, -1), "w_in": f(w_in)[0],
        "w_proj_attn": f(w_proj_attn)[0], "g_sgu": f(g_sgu).reshape(1, -1), "w_sgu": f(w_sgu)[0], "b_sgu": f(b_sgu)[0],
        "w_proj_sgu": f(w_proj_sgu)[0], "w_out": f(w_out)[0], "g_ffn": f(g_ffn).reshape(1, -1), "w_ff1": f(w_ff1)[0],
        "w_ff2": f(w_ff2)[0], "g_final": f(g_final).reshape(1, -1),
    }
    shared.update(make_consts(NB))
    if NB not in _NC_CACHE:
        _NC_CACHE[NB] = build_nc(NB)
    nc = _NC_CACHE[NB]
    in_maps = []
    for i in range(NCORES):
        m = dict(shared)
        m["x"] = x[i * NB:(i + 1) * NB]
        m["c"] = c[i * NB:(i + 1) * NB]
        in_maps.append(m)
    res = run_bass_kernel_spmd(nc, in_maps, core_ids=list(range(NCORES)))
    return np.concatenate([r["out"] for r in res.results], axis=0)
```
